# Optimizing a Trainium2 kernel written in Bass

```python
import jax, jax.numpy as jnp
from jax import lax
import numpy as np

D_MODEL = 1024
BATCH = 2
SEQ = 16384
DEPTH = 4

CHUNK = 64
SGU_BLOCK = 128
D_A = 1024
A_GROUPS = 8
A_GROUP_DIM = D_A // A_GROUPS
D_B = 1024
POOL_WINDOWS = (2, 4, 8, 16)
B_GROUPS = len(POOL_WINDOWS)
B_GROUP_DIM = D_B // B_GROUPS
D_C = 1024
CONV_WIDTH = 3
N_BRANCH = 3
SPLIT_SIZES = (D_A, D_A, D_A, D_B, D_B, D_C, D_C, D_C, D_C, N_BRANCH * D_MODEL)
N_IN = sum(SPLIT_SIZES)
SPLIT_OFFSETS = tuple(int(o) for o in np.cumsum(SPLIT_SIZES)[:-1])
RMS_EPS = 1e-6
LN_EPS = 1e-5

kernel_name = "hybrid_gated_parallel_mixers"


def rmsnorm(x, g):
    xf = x.astype(jnp.float32)
    y = xf * lax.rsqrt(jnp.mean(xf * xf, axis=-1, keepdims=True) + RMS_EPS)
    return (y * g.astype(jnp.float32)).astype(x.dtype)


def layernorm(x, g, b):
    xf = x.astype(jnp.float32)
    mu = jnp.mean(xf, axis=-1, keepdims=True)
    var = jnp.mean(jnp.square(xf - mu), axis=-1, keepdims=True)
    y = (xf - mu) * lax.rsqrt(var + LN_EPS)
    return (y * g.astype(jnp.float32) + b.astype(jnp.float32)).astype(x.dtype)


def chunk_causal_mask():
    c = jnp.arange(SGU_BLOCK) // CHUNK
    return c[None, :] <= c[:, None]


def spatial_gating(u, v, w_s, b_s, ln_g, ln_b):
    bsz, s, _ = v.shape
    v = layernorm(v, ln_g, ln_b)
    vb = v.reshape(bsz, s // SGU_BLOCK, SGU_BLOCK, A_GROUPS, A_GROUP_DIM)
    w = jnp.where(chunk_causal_mask()[None], w_s, jnp.zeros_like(w_s))
    mixed = jnp.einsum('gij,bnjgc->bnigc', w, vb) + b_s.T[:, :, None]
    return u * mixed.reshape(bsz, s, D_A)


def multiscale_pool(p, w_g, b_g, scale):
    bsz, s, _ = p.shape
    pf = p.astype(jnp.float32)
    csum = jnp.cumsum(pf, axis=1)
    pos1 = jnp.arange(1, s + 1, dtype=jnp.int32)
    outs = []
    for k, win in enumerate(POOL_WINDOWS):
        sl = slice(k * B_GROUP_DIM, (k + 1) * B_GROUP_DIM)
        cg = csum[..., sl]
        shifted = jnp.pad(cg, ((0, 0), (win, 0), (0, 0)))[:, :s]
        count = jnp.minimum(pos1, win).astype(jnp.float32)[:, None]
        outs.append((cg - shifted) / count - pf[..., sl])
    d = jnp.concatenate(outs, axis=-1).astype(p.dtype)
    d = d.reshape(bsz, s, B_GROUPS, B_GROUP_DIM)
    y = jnp.einsum('bsgc,gcd->bsgd', d, w_g).reshape(bsz, s, D_B) + b_g
    return y * scale


def causal_depthwise_conv(h, w, b):
    s = h.shape[1]
    hp = jnp.pad(h, ((0, 0), (CONV_WIDTH - 1, 0), (0, 0)))
    y = sum(w[k] * hp[:, k:k + s] for k in range(CONV_WIDTH))
    return y + b


def setup_inputs(seed: int = 0) -> dict:
    key = jax.random.key(seed)
    ks = jax.random.split(key, 20)

    def nrm(k, shape, scale):
        return jax.random.normal(k, shape, jnp.float32) * scale

    L = DEPTH
    return {
        "x": nrm(ks[0], (BATCH, SEQ, D_MODEL), 1.0),
        "norm_g": 1.0 + nrm(ks[1], (L, D_MODEL), 0.05),
        "w_in": nrm(ks[2], (L, D_MODEL, N_IN), D_MODEL ** -0.5),
        "sgu_ln_g": 1.0 + nrm(ks[3], (L, D_A), 0.05),
        "sgu_ln_b": nrm(ks[4], (L, D_A), 0.02),
        "sgu_w": nrm(ks[5], (L, A_GROUPS, SGU_BLOCK, SGU_BLOCK), SGU_BLOCK ** -0.5),
        "sgu_b": 1.0 + nrm(ks[6], (L, A_GROUPS, SGU_BLOCK), 0.05),
        "pool_w": nrm(ks[7], (L, B_GROUPS, B_GROUP_DIM, B_GROUP_DIM), B_GROUP_DIM ** -0.5),
        "pool_b": nrm(ks[8], (L, D_B), 0.01),
        "pool_scale": 1.0 + nrm(ks[9], (L, D_B), 0.1),
        "conv_w": nrm(ks[10], (L, CONV_WIDTH, D_C), CONV_WIDTH ** -0.5),
        "conv_b": nrm(ks[11], (L, D_C), 0.01),
        "w_branch_a": nrm(ks[12], (L, D_A, D_MODEL), D_A ** -0.5),
        "w_branch_b": nrm(ks[13], (L, D_B, D_MODEL), D_B ** -0.5),
        "w_branch_c": nrm(ks[14], (L, D_C, D_MODEL), D_C ** -0.5),
        "w_out": nrm(ks[15], (L, D_MODEL, D_MODEL), D_MODEL ** -0.5),
        "final_g": 1.0 + nrm(ks[16], (D_MODEL,), 0.05),
    }


def reference(x, norm_g, w_in, sgu_ln_g, sgu_ln_b, sgu_w, sgu_b, pool_w, pool_b,
              pool_scale, conv_w, conv_b, w_branch_a, w_branch_b, w_branch_c,
              w_out, final_g):
    bsz, s, _ = x.shape
    for l in range(DEPTH):
        h = rmsnorm(x, norm_g[l])
        proj = jnp.einsum('bsd,dn->bsn', h, w_in[l])
        a_u, a_v, a_z, b_p, b_z, c_h, c_b, c_c, c_z, gate_logits = jnp.split(
            proj, SPLIT_OFFSETS, axis=-1)

        ya = spatial_gating(jax.nn.gelu(a_u), jax.nn.gelu(a_v), sgu_w[l], sgu_b[l],
                            sgu_ln_g[l], sgu_ln_b[l]) * jax.nn.silu(a_z)
        yb = multiscale_pool(b_p, pool_w[l], pool_b[l], pool_scale[l]) * jax.nn.silu(b_z)
        yc = c_b * causal_depthwise_conv(c_c * c_h, conv_w[l], conv_b[l]) * jax.nn.silu(c_z)

        gates = jax.nn.sigmoid(gate_logits.reshape(bsz, s, N_BRANCH, D_MODEL))
        merged = (gates[:, :, 0] * jnp.einsum('bsc,cd->bsd', ya, w_branch_a[l])
                  + gates[:, :, 1] * jnp.einsum('bsc,cd->bsd', yb, w_branch_b[l])
                  + gates[:, :, 2] * jnp.einsum('bsc,cd->bsd', yc, w_branch_c[l]))
        x = x + jnp.einsum('bsd,de->bse', merged, w_out[l])
    return rmsnorm(x, final_g)
```

```python
from contextlib import ExitStack

import numpy as np
import concourse.bass as bass
import concourse.mybir as mybir
from concourse.bass_utils import run_bass_kernel_spmd

F32 = mybir.dt.float32
BF16 = mybir.dt.bfloat16
AF = mybir.ActivationFunctionType
ALU = mybir.AluOpType

D = 1024
L = 4
NIN = 12288
NV = 8
RMS_EPS = 1e-6
LN_EPS = 1e-5
NSLOT = 4
HALO_BLK = 2
TRIM = True
FUSED = True

BLK_AU, BLK_AV, BLK_AZ, BLK_BP, BLK_BZ, BLK_CH, BLK_CB, BLK_CC, BLK_CZ, BLK_G = 0, 2, 4, 6, 8, 10, 12, 14, 16, 18
V_NG, V_LNG, V_PB, V_PS, V_CW0, V_CW1, V_CW2, V_CB = range(8)


class Buf:
    __slots__ = ("w", "r")

    def __init__(self):
        self.w = None
        self.r = {}


class Sched:
    ENGS = ("pe", "act", "dve", "pool", "sp")

    def __init__(self):
        self.q = {e: [] for e in self.ENGS}
        self.cnt = {}
        self.seen = {e: {} for e in self.ENGS}

    def op(self, eng, fn, reads=(), writes=(), extra=(), sem=None):
        key = sem if sem is not None else eng
        inc = 16 if sem is not None else 1
        deps = {}

        def add(t):
            if t is None:
                return
            k, n = t
            if deps.get(k, 0) < n:
                deps[k] = n

        for b in reads:
            add(b.w)
        for b in writes:
            add(b.w)
            for k, n in b.r.items():
                add((k, n))
        for t in extra:
            add(t)
        waits = []
        seen = self.seen[eng]
        for k, n in deps.items():
            if eng == "pe" and k == "pe":
                continue
            if seen.get(k, 0) < n:
                waits.append((k, n))
                seen[k] = n
        self.cnt[key] = self.cnt.get(key, 0) + inc
        tok = (key, self.cnt[key])
        self.q[eng].append((fn, waits, key, inc))
        for b in reads:
            if b.r.get(key, 0) < tok[1]:
                b.r[key] = tok[1]
        for b in writes:
            b.w = tok
            b.r = {}
        return tok


def build(tiles, n_layers, final_norm, LW, fix_col=HALO_BLK * 128, out_skip=HALO_BLK * 128):
    nc = bass.Bass("TRN2", target_bir_lowering=False)
    NBLK = sum(tiles)
    NTOK = NBLK * 128
    TB = max(tiles)
    TT = TB * 128
    NOUT = NTOK - out_skip

    xT_d = nc.dram_tensor("xT", [D, NTOK], F32, kind="ExternalInput").ap()
    w_in_d = nc.dram_tensor("w_in", [LW, D, NIN], F32, kind="ExternalInput").ap()
    w_br_d = nc.dram_tensor("w_br", [LW, 3, D, D], F32, kind="ExternalInput").ap()
    w_out_d = nc.dram_tensor("w_out", [LW, D, D], F32, kind="ExternalInput").ap()
    pvec_d = nc.dram_tensor("pvec", [128, LW * NV * 8], F32, kind="ExternalInput").ap()
    fing_d = nc.dram_tensor("fing", [128, 8], F32, kind="ExternalInput").ap()
    sguT_d = nc.dram_tensor("sgu_wT", [LW, 128, 1024], F32, kind="ExternalInput").ap()
    rows_d = nc.dram_tensor("rows", [LW, 2, 1024], F32, kind="ExternalInput").ap()
    lngbc_d = nc.dram_tensor("lng_bc", [LW, 128, 1024], F32, kind="ExternalInput").ap()
    poolw_d = nc.dram_tensor("pool_w", [LW, 4, 256, 256], F32, kind="ExternalInput").ap()
    invc_d = nc.dram_tensor("invcnt", [128, 64], F32, kind="ExternalInput").ap()
    outT_d = nc.dram_tensor("outT", [D, NOUT], F32, kind="ExternalOutput").ap()

    xT_v = xT_d.rearrange("(k p) t -> p k t", p=128)
    outT_v = outT_d.rearrange("(k p) t -> p k t", p=128)

    S = Sched()
    es = ExitStack()

    def sb(name, shape, dt):
        return es.enter_context(nc.sbuf_tensor(name, shape, dt))

    NSTG = 8
    bufs2 = [sb("xa", [128, 8, TT], F32), sb("xc", [128, 8, TT], F32)]
    hT = sb("hT", [128, 8, TT], BF16)
    vm = sb("vm", [128, TB * 1024], BF16)
    yb = [sb("y0", [128, 8, TT], BF16), sb("y1", [128, 8, TT], BF16)]
    ring = sb("ring", [128, NSLOT, 8, 512], BF16)
    stg = sb("stg", [128, NSTG, 528], F32)
    stgb = sb("stgb", [128, 2, 512], BF16)
    gv = sb("gv", [128, 2, 1024], F32)
    gb = sb("gb", [128, 2, 516], F32)
    pbuf = sb("pbuf", [128, 3, 528], F32)
    dgrp = sb("dgrp", [128, 4, 2, 512], BF16)
    swT = sb("swT", [128, 1024], BF16)
    lngbc = sb("lngbc", [128, 1024], F32)
    pw = sb("pw", [128, 4, 2, 256], BF16)
    lb2 = sb("lb2", [2, 1024], BF16)
    rb2 = sb("rb2", [2, 1024], BF16)
    pv = sb("pv", [128, LW * NV * 8], F32)
    fg = sb("fg", [128, 8], F32)
    ic = sb("ic", [128, 64], F32)
    epsr = sb("epsr", [128, 2], F32)
    bsc = sb("bsc", [128, LW * 8], F32)
    onesb = sb("onesb", [128, 128], BF16)
    pst = sb("pst", [128, LW, 8, 16], F32)
    cst = sb("cst", [128, LW, 8, 2], F32)
    sm = sb("sm", [128, 4, 16], F32)
    ps = es.enter_context(nc.psum_tensor("ps", [128, 8, 512], F32))

    grids2 = [[[Buf() for _ in range(2)] for _ in range(8)] for _ in range(2)]
    cur = {}

    def set_tile(ti_):
        cur["xb"], cur["X"] = bufs2[ti_ % 2], grids2[ti_ % 2]
        cur["macc"], cur["M"] = bufs2[1 - ti_ % 2], grids2[1 - ti_ % 2]
    H = [[Buf() for _ in range(2)] for _ in range(8)]
    V = [Buf() for _ in range(TB)]
    MR = [[Buf() for _ in range(2)] for _ in range(8)]
    Y = [[[Buf() for _ in range(2)] for _ in range(8)] for _ in range(2)]
    RB = [Buf() for _ in range(NSLOT)]
    STG = [Buf() for _ in range(NSTG)]
    STGB = [Buf() for _ in range(2)]
    GV = [Buf() for _ in range(2)]
    GB = [Buf() for _ in range(2)]
    PBUF = [Buf() for _ in range(3)]
    DG = [[Buf() for _ in range(2)] for _ in range(4)]
    SWT, PW, LB, RB2, LNG = Buf(), Buf(), Buf(), Buf(), Buf()
    CONST = Buf()
    PST = [[Buf() for _ in range(8)] for _ in range(LW)]
    CST = [[Buf() for _ in range(8)] for _ in range(LW)]
    SM = [Buf() for _ in range(4)]
    PB = [Buf() for _ in range(8)]

    rot = {"stg": 0, "stgb": 0, "gv": 0, "gb": 0, "pbuf": 0, "dg": 0, "sm": 0, "pb": 0}

    def nxt(name, n):
        i = rot[name] % n
        rot[name] += 1
        return i

    def bank():
        return nxt("pb", 8)

    def nstg():
        return nxt("stg", NSTG)

    def pvcol(l, v, k):
        c = (l * NV + v) * 8 + k
        return pv[:, c:c + 1]

    def vtok(b):
        return vm[:, b * 1024:(b + 1) * 1024]

    def mrg(e, c0, n):
        return vm[:, e * TT + c0:e * TT + c0 + n]

    blocks = []

    def w_in_blk(l, c):
        return w_in_d[l].rearrange("(k p) n -> p k n", p=128)[:, :, c * 512:(c + 1) * 512]

    def w_br_blk(l, br, hq):
        return w_br_d[l, br].rearrange("(k p) n -> p k n", p=128)[:, :, hq * 512:(hq + 1) * 512]

    def w_out_blk(l, hq):
        return w_out_d[l].rearrange("(k p) n -> p k n", p=128)[:, :, hq * 512:(hq + 1) * 512]

    def layer_blocks(l):
        out = []
        for hq in range(2):
            out += [w_in_blk(l, BLK_CH + hq), w_in_blk(l, BLK_CC + hq),
                    w_in_blk(l, BLK_CZ + hq), w_in_blk(l, BLK_CB + hq)]
        for hq in range(2):
            out += [w_in_blk(l, BLK_BP + hq), w_in_blk(l, BLK_BZ + hq)]
        for hq in range(2):
            out += [w_in_blk(l, BLK_G + 4 + hq), w_br_blk(l, 2, hq)]
        out += [w_in_blk(l, BLK_AV), w_in_blk(l, BLK_AV + 1)]
        for hq in range(2):
            out += [w_in_blk(l, BLK_AU + hq), w_in_blk(l, BLK_AZ + hq)]
        for hq in range(2):
            out += [w_in_blk(l, BLK_G + 2 + hq), w_br_blk(l, 1, hq)]
        for hq in range(2):
            out += [w_in_blk(l, BLK_G + hq), w_br_blk(l, 0, hq)]
        out += [w_out_blk(l, 0), w_out_blk(l, 1)]
        return out

    for _ti in range(len(tiles)):
        for _l in range(n_layers):
            blocks.extend(layer_blocks(_l))
    ring_state = {"issued": 0, "acq": 0}

    def ring_issue():
        i = ring_state["issued"]
        if i >= len(blocks):
            return
        slot = i % NSLOT
        src = blocks[i]
        S.op("pool", lambda g, slot=slot, src=src: g.dma_start(out=ring[:, slot], in_=src),
             writes=[RB[slot]], sem="ring%d" % slot)
        ring_state["issued"] += 1

    def acquire():
        i = ring_state["acq"]
        while ring_state["issued"] <= i:
            ring_issue()
        ring_state["acq"] += 1
        return i % NSLOT

    live = set()

    def pump():
        while ring_state["issued"] < len(blocks):
            nslot = ring_state["issued"] % NSLOT
            if nslot in live or ring_state["issued"] - ring_state["acq"] >= NSLOT - len(live):
                break
            ring_issue()

    def acq():
        s_ = acquire()
        live.add(s_)
        return s_

    def rel(s_):
        live.discard(s_)
        pump()

    def mm_group(bk, n, lhs_list, rhs_list, reads, col0=0, prow=None):
        def fn(pe):
            ins = None
            last = len(lhs_list) - 1
            for idx in range(len(lhs_list)):
                o = ps[:, bk, col0:col0 + n] if prow is None else ps[0:prow, bk, col0:col0 + n]
                ins = pe.matmul(o, lhsT=lhs_list[idx], rhs=rhs_list[idx], start=(idx == 0), stop=(idx == last))
            return ins
        return S.op("pe", fn, reads=reads, writes=[PB[bk]])

    def proj_group(slot, cc, src, SRC, si, c0, n):
        bk = bank()
        lhs = [ring[:, slot, k, cc * 128:(cc + 1) * 128] for k in range(8)]
        rhs = [src[:, k, c0:c0 + n] for k in range(8)]
        mm_group(bk, n, lhs, rhs, [RB[slot]] + [SRC[k][si] for k in range(8)])
        return bk

    def act(out, in_, func, reads, writes, bias=None, scale=None):
        kw = {}
        if bias is not None:
            kw["bias"] = bias
        if scale is not None:
            kw["scale"] = scale
        return S.op("act", lambda a: a.activation(out=out, in_=in_, func=func, **kw), reads=reads, writes=writes)

    def tt(out, in0, in1, op, reads, writes, extra=()):
        return S.op("dve", lambda v: v.tensor_tensor(out=out, in0=in0, in1=in1, op=op), reads=reads, writes=writes,
                    extra=extra)

    def ptt(out, in0, in1, op, reads, writes, extra=()):
        return S.op("pool", lambda g: g.tensor_tensor(out=out, in0=in0, in1=in1, op=op), reads=reads, writes=writes,
                    extra=extra)

    def stt(out, in0, scalar, in1, op0, op1, reads, writes, extra=()):
        return S.op("dve", lambda v: v.scalar_tensor_tensor(out=out, in0=in0, scalar=scalar, in1=in1, op0=op0, op1=op1),
                    reads=reads, writes=writes, extra=extra)

    def ts(out, in0, s1, s2, op0, op1, reads, writes, extra=()):
        return S.op("dve", lambda v: v.tensor_scalar(out=out, in0=in0, scalar1=s1, scalar2=s2, op0=op0, op1=op1),
                    reads=reads, writes=writes, extra=extra)

    def cp(out, in_, reads, writes):
        return S.op("dve", lambda v: v.tensor_copy(out=out, in_=in_), reads=reads, writes=writes)

    S.op("sp", lambda q: q.dma_start(out=pv[:], in_=pvec_d[:, :]), writes=[CONST], sem="const")
    S.op("sp", lambda q: q.dma_start(out=fg[:], in_=fing_d[:, :]), writes=[CONST], sem="const")
    S.op("sp", lambda q: q.dma_start(out=ic[:], in_=invc_d[:, :]), writes=[CONST], sem="const")
    S.op("dve", lambda v: v.memset(onesb[:], 1.0), writes=[CONST])
    S.op("dve", lambda v: v.memset(lb2[:], 1.0), writes=[LB])
    S.op("dve", lambda v: v.memset(rb2[:], 0.0), writes=[RB2])
    S.op("dve", lambda v: v.memset(pst[:], 0.0), writes=[b for r in PST for b in r])
    S.op("dve", lambda v: v.memset(cst[:], 0.0), writes=[b for r in CST for b in r])
    for l in range(n_layers):
        pbc = (l * NV + V_PB) * 8
        psc = (l * NV + V_PS) * 8
        tt(bsc[:, l * 8:(l + 1) * 8], pv[:, pbc:pbc + 8], pv[:, psc:psc + 8], ALU.mult, [CONST], [CONST])
    S.op("dve", lambda v: v.memset(epsr[:, 0:1], RMS_EPS), writes=[CONST])
    S.op("dve", lambda v: v.memset(epsr[:, 1:2], LN_EPS), writes=[CONST])

    alias = {"last_sgu": None, "last_g": None}

    def load_small_dma(l):
        t = S.op("pool", lambda g: g.dma_start(out=swT[:], in_=sguT_d[l]), writes=[SWT], sem="sw")
        for g2 in range(4):
            t = S.op("pool", lambda g, g2=g2: g.dma_start(
                out=pw[:, g2], in_=poolw_d[l, g2].rearrange("(k p) d -> p k d", p=128)), writes=[PW], sem="sw")
        t = S.op("pool", lambda g: g.dma_start(out=lb2[0:1, :], in_=rows_d[l, 0:1, :]), writes=[LB], sem="sw")
        t = S.op("pool", lambda g: g.dma_start(out=rb2[1:2, :], in_=rows_d[l, 1:2, :]), writes=[RB2], sem="sw")
        t = S.op("pool", lambda g: g.dma_start(out=lngbc[:], in_=lngbc_d[l]), writes=[LNG], sem="sw")
        for b_ in (SWT, PW, LB, RB2, LNG):
            b_.w = t

    def load_small_finish(l):
        for g in range(8):
            S.op("dve", lambda v, g=g: v.memset(swT[64:128, g * 128:g * 128 + 64], 0.0), writes=[SWT])
        for half in range(2):
            bk = bank()

            def fn(pe, half=half, bk=bk):
                ins = None
                for gi in range(4):
                    g = half * 4 + gi
                    ins = pe.matmul(ps[0:1, bk, gi * 128:(gi + 1) * 128], lhsT=onesb[:, 0:1],
                                    rhs=swT[:, g * 128:(g + 1) * 128], start=True, stop=True)
                return ins
            S.op("pe", fn, reads=[SWT, CONST], writes=[PB[bk]])
            act(rb2[0:1, half * 512:(half + 1) * 512], ps[0:1, bk, 0:512], AF.Copy, [PB[bk]], [RB2])

    def rms_stats(si, c0, n):
        xb, X = cur["xb"], cur["X"]
        bk = bank()
        for k in range(8):
            sq = nxt("stgb", 2)
            act(stgb[:, sq, 0:n], xb[:, k, c0:c0 + n], AF.Square, [X[k][si]], [STGB[sq]])

            def fn(pe, k=k, sq=sq):
                return pe.matmul(ps[:, bk, 0:n], lhsT=onesb[:], rhs=stgb[:, sq, 0:n], start=(k == 0), stop=(k == 7))
            S.op("pe", fn, reads=[STGB[sq], CONST], writes=[PB[bk]])
        lnv = nstg()
        act(stg[:, lnv, 0:n], ps[:, bk, 0:n], AF.Ln, [PB[bk], CONST], [STG[lnv]], bias=epsr[:, 0:1], scale=1.0 / 1024.0)
        rr = nstg()
        act(stg[:, rr, 0:n], stg[:, lnv, 0:n], AF.Exp, [STG[lnv]], [STG[rr]], scale=-0.5)
        return rr

    def rms_apply(l, si, c0, n, final, rr, k):
        xb, X = cur["xb"], cur["X"]
        if final:
            stt(xb[:, k, c0:c0 + n], xb[:, k, c0:c0 + n], fg[:, k:k + 1], stg[:, rr, 0:n], ALU.mult, ALU.mult,
                [X[k][si], STG[rr], CONST], [X[k][si]])
        else:
            stt(hT[:, k, c0:c0 + n], xb[:, k, c0:c0 + n], pvcol(l, V_NG, k), stg[:, rr, 0:n],
                ALU.mult, ALU.mult, [X[k][si], STG[rr], CONST], [H[k][si]])

    def rmsnorm(l, si, c0, n, final):
        rr = rms_stats(si, c0, n)
        for k in range(8):
            rms_apply(l, si, c0, n, final, rr, k)

    def prod_c(l, yi, subs, so_col=None):
        macc, M = cur["macc"], cur["M"]
        for hq in range(2):
            sH = acq()
            sC = acq()
            if so_col is not None:
                for jj in range(4):
                    j = 4 * hq + jj
                    bh = proj_group(sH, jj, hT, H, 0, so_col, 16)
                    bc = proj_group(sC, jj, hT, H, 0, so_col, 16)
                    ch = nstg()
                    act(stg[:, ch, 0:16], ps[:, bh, 0:16], AF.Copy, [PB[bh]], [STG[ch]])
                    tt(cst[:, l, j, :], ps[:, bc, 14:16], stg[:, ch, 14:16], ALU.mult, [PB[bc], STG[ch]],
                       [CST[l][j]])
            for si, (c0, n) in enumerate(subs):
                for jj in range(4):
                    j = 4 * hq + jj
                    bh = proj_group(sH, jj, hT, H, si, c0, n)
                    bc = proj_group(sC, jj, hT, H, si, c0, n)
                    ch = nstg()
                    act(stg[:, ch, 0:n], ps[:, bh, 0:n], AF.Copy, [PB[bh]], [STG[ch]])
                    g_ = nxt("gb", 2)
                    cp(gb[:, g_, 0:2], cst[:, l, j, :], [CST[l][j]], [GB[g_]])
                    tt(gb[:, g_, 2:2 + n], ps[:, bc, 0:n], stg[:, ch, 0:n], ALU.mult, [PB[bc], STG[ch]], [GB[g_]])
                    cp(cst[:, l, j, :], gb[:, g_, n:n + 2], [GB[g_]], [CST[l][j]])
                    t0 = nstg()
                    act(stg[:, t0, 0:n], gb[:, g_, 2:2 + n], AF.Identity, [GB[g_], CONST], [STG[t0]],
                        bias=pvcol(l, V_CB, j), scale=pvcol(l, V_CW2, j))
                    t1 = nstg()
                    stt(stg[:, t1, 0:n], gb[:, g_, 1:1 + n], pvcol(l, V_CW1, j), stg[:, t0, 0:n], ALU.mult, ALU.add,
                        [GB[g_], STG[t0], CONST], [STG[t1]])
                    stt(macc[:, jj, c0:c0 + n], gb[:, g_, 0:n], pvcol(l, V_CW0, j), stg[:, t1, 0:n], ALU.mult, ALU.add,
                        [GB[g_], STG[t1], CONST], [M[jj][si]])
            rel(sH)
            rel(sC)
            sZ = acq()
            sB = acq()
            for si, (c0, n) in enumerate(subs):
                for jj in range(4):
                    j = 4 * hq + jj
                    bz = proj_group(sZ, jj, hT, H, si, c0, n)
                    bb = proj_group(sB, jj, hT, H, si, c0, n)
                    sz = nstg()
                    act(stg[:, sz, 0:n], ps[:, bz, 0:n], AF.Silu, [PB[bz]], [STG[sz]])
                    t3 = nstg()
                    tt(stg[:, t3, 0:n], ps[:, bb, 0:n], stg[:, sz, 0:n], ALU.mult, [PB[bb], STG[sz]], [STG[t3]])
                    tt(yb[yi][:, j, c0:c0 + n], stg[:, t3, 0:n], macc[:, jj, c0:c0 + n], ALU.mult,
                       [STG[t3], M[jj][si]], [Y[yi][j][si]])
            rel(sZ)
            rel(sB)

    def prod_b(l, yi, subs, ti, so_col=None):
        for hq in range(2):
            sP = acq()
            sZ = acq()
            if so_col is not None:
                for jj in range(4):
                    j = 4 * hq + jj
                    bk = proj_group(sP, jj, hT, H, 0, so_col, 16)
                    act(pst[:, l, j, :], ps[:, bk, 0:16], AF.Copy, [PB[bk]], [PST[l][j]])
            items = [(gg, si, c0, n) for gg in range(2) for si, (c0, n) in enumerate(subs)]
            dsets = {}

            def part1(gg, si, c0, n):
                g2 = 2 * hq + gg
                w = 2 ** (g2 + 1)
                dr = nxt("dg", 4)
                dsets[(gg, si)] = dr
                for kc in range(2):
                    j = 2 * g2 + kc
                    jj = 2 * gg + kc
                    bk = proj_group(sP, jj, hT, H, si, c0, n)
                    pb_ = nxt("pbuf", 3)
                    P = pbuf[:, pb_]
                    cp(P[:, 0:16], pst[:, l, j, :], [PST[l][j]], [PBUF[pb_]])
                    act(P[:, 16:16 + n], ps[:, bk, 0:n], AF.Copy, [PB[bk]], [PBUF[pb_]])
                    cp(pst[:, l, j, :], P[:, n:n + 16], [PBUF[pb_]], [PST[l][j]])
                    a_ = nstg()
                    A = stg[:, a_]
                    addop = tt if kc == 0 else ptt
                    addop(A[:, 1:16 + n], P[:, 1:16 + n], P[:, 0:15 + n], ALU.add, [PBUF[pb_]], [STG[a_]])
                    cur_, curb = A, a_
                    sh, lo = 2, 3
                    while sh < w:
                        b_ = nstg()
                        Bv = stg[:, b_]
                        addop(Bv[:, lo:16 + n], cur_[:, lo:16 + n], cur_[:, lo - sh:16 + n - sh], ALU.add,
                              [STG[curb]], [STG[b_]])
                        cur_, curb = Bv, b_
                        sh *= 2
                        lo = 2 * sh - 1
                    stt(dgrp[:, dr, kc, 0:n], cur_[:, 16:16 + n], 1.0 / w, P[:, 16:16 + n], ALU.mult, ALU.subtract,
                        [STG[curb], PBUF[pb_]], [DG[dr][kc]])
                    if ti == 0 and c0 <= fix_col < c0 + n:
                        f0 = fix_col - c0
                        tf = nstg()
                        tt(stg[:, tf, 0:16], cur_[:, 16 + f0:32 + f0], ic[:, g2 * 16:(g2 + 1) * 16], ALU.mult,
                           [STG[curb], CONST], [STG[tf]])
                        tt(dgrp[:, dr, kc, f0:f0 + 16], stg[:, tf, 0:16], P[:, 16 + f0:32 + f0], ALU.subtract,
                           [STG[tf], PBUF[pb_]], [DG[dr][kc]])

            def part2(gg, si, c0, n):
                g2 = 2 * hq + gg
                dr = dsets[(gg, si)]
                for oc in range(2):
                    j = 2 * g2 + oc
                    jj = 2 * gg + oc
                    by = bank()
                    mm_group(by, n, [pw[:, g2, kc, oc * 128:(oc + 1) * 128] for kc in range(2)],
                             [dgrp[:, dr, kc, 0:n] for kc in range(2)], [PW, DG[dr][0], DG[dr][1]])
                    bz = proj_group(sZ, jj, hT, H, si, c0, n)
                    sz = nstg()
                    act(stg[:, sz, 0:n], ps[:, bz, 0:n], AF.Silu, [PB[bz]], [STG[sz]])
                    ya = nstg()
                    act(stg[:, ya, 0:n], ps[:, by, 0:n], AF.Identity, [PB[by], CONST], [STG[ya]],
                        bias=bsc[:, l * 8 + j:l * 8 + j + 1], scale=pvcol(l, V_PS, j))
                    tt(yb[yi][:, j, c0:c0 + n], stg[:, ya, 0:n], stg[:, sz, 0:n], ALU.mult,
                       [STG[ya], STG[sz]], [Y[yi][j][si]])

            SK = 2
            for idx, it in enumerate(items):
                part1(*it)
                if idx >= SK:
                    part2(*items[idx - SK])
            for it in items[max(0, len(items) - SK):]:
                part2(*it)
            rel(sP)
            rel(sZ)

    def prod_a(l, yi, subs, nb, blo=0):
        s0_ = acq()
        s1_ = acq()
        for bp0 in range(blo, nb, 2):
            grp = list(range(bp0, min(bp0 + 2, nb)))
            info = {}
            for b in grp:
                si = 0 if b < 4 else 1
                bA = bank()
                bB = bank()
                for half, (bk, sl) in enumerate(((bA, s0_), (bB, s1_))):
                    mm_group(bk, 512, [hT[:, k, b * 128:(b + 1) * 128] for k in range(8)],
                             [ring[:, sl, k, :] for k in range(8)], [RB[sl]] + [H[k][si] for k in range(8)])
                gi = nxt("gv", 2)
                m_ = nxt("sm", 4)
                info[b] = (gi, m_)
                act(gv[:, gi, 0:512], ps[:, bA, :], AF.Gelu_apprx_tanh, [PB[bA]], [GV[gi]])
                act(gv[:, gi, 512:1024], ps[:, bB, :], AF.Gelu_apprx_tanh, [PB[bB]], [GV[gi]])
                S.op("dve", lambda v, gi=gi, m_=m_: v.bn_stats(out=sm[:, m_, 0:6], in_=gv[:, gi, 0:512]),
                     reads=[GV[gi]], writes=[SM[m_]])
                S.op("dve", lambda v, gi=gi, m_=m_: v.bn_stats(out=sm[:, m_, 6:12], in_=gv[:, gi, 512:1024]),
                     reads=[GV[gi]], writes=[SM[m_]])
                S.op("dve", lambda v, m_=m_: v.bn_aggr(out=sm[:, m_, 12:14], in_=sm[:, m_, 0:12]),
                     reads=[SM[m_]], writes=[SM[m_]])
            for b in grp:
                gi, m_ = info[b]
                act(sm[:, m_, 14:15], sm[:, m_, 13:14], AF.Sqrt, [SM[m_], CONST], [SM[m_]], bias=epsr[:, 1:2])
            for b in grp:
                gi, m_ = info[b]
                S.op("dve", lambda v, m_=m_: v.reciprocal(out=sm[:, m_, 15:16], in_=sm[:, m_, 14:15]),
                     reads=[SM[m_]], writes=[SM[m_]])
                ts(gv[:, gi, :], gv[:, gi, :], sm[:, m_, 12:13], sm[:, m_, 15:16], ALU.subtract, ALU.mult,
                   [GV[gi], SM[m_]], [GV[gi]])
                tt(vtok(b), gv[:, gi, :], lngbc[:], ALU.mult, [GV[gi], LNG], [V[b]], extra=[alias["last_g"]])
        rel(s0_)
        rel(s1_)
        for hq in range(2):
            sU = acq()
            sZ = acq()
            for si, (c0, n) in enumerate(subs):
                for gg in range(4):
                    g = 4 * hq + gg
                    bm = bank()
                    b0 = c0 // 128
                    nbk = n // 128

                    def fn(pe, bm=bm, b0=b0, nbk=nbk, g=g):
                        ins = None
                        for bi in range(nbk):
                            o = ps[:, bm, bi * 128:(bi + 1) * 128]
                            pe.matmul(o, lhsT=vtok(b0 + bi)[:, g * 128:(g + 1) * 128], rhs=swT[:, g * 128:(g + 1) * 128],
                                      start=True, stop=False)
                            ins = pe.matmul(o, lhsT=lb2[0:2, g * 128:(g + 1) * 128],
                                            rhs=rb2[0:2, g * 128:(g + 1) * 128], start=False, stop=True)
                        return ins
                    alias["last_sgu"] = S.op("pe", fn, reads=[V[b0 + bi] for bi in range(nbk)] + [SWT, LB, RB2],
                                             writes=[PB[bm]])
                    bu = proj_group(sU, gg, hT, H, si, c0, n)
                    bz = proj_group(sZ, gg, hT, H, si, c0, n)
                    gu = nstg()
                    act(stg[:, gu, 0:n], ps[:, bu, 0:n], AF.Gelu_apprx_tanh, [PB[bu]], [STG[gu]])
                    th = nstg()
                    act(stg[:, th, 0:n], ps[:, bz, 0:n], AF.Tanh, [PB[bz]], [STG[th]], scale=0.5)
                    t = nstg()
                    tt(stg[:, t, 0:n], ps[:, bm, 0:n], stg[:, gu, 0:n], ALU.mult, [PB[bm], STG[gu]], [STG[t]])
                    t2 = nstg()
                    stt(stg[:, t2, 0:n], stg[:, th, 0:n], 1.0, stg[:, t, 0:n], ALU.add, ALU.mult,
                        [STG[th], STG[t]], [STG[t2]])
                    stt(yb[yi][:, g, c0:c0 + n], ps[:, bz, 0:n], 0.5, stg[:, t2, 0:n], ALU.mult, ALU.mult,
                        [PB[bz], STG[t2]], [Y[yi][g][si]])
            rel(sU)
            rel(sZ)

    def phase_f(l, br, yi, subs, mode):
        macc, M = cur["macc"], cur["M"]
        for hq in range(2):
            sG = acq()
            sW = acq()
            for ee in range(4):
                e = 4 * hq + ee
                for si, (c0, n) in enumerate(subs):
                    bg = proj_group(sG, ee, hT, H, si, c0, n)
                    bp = proj_group(sW, ee, yb[yi], Y[yi], si, c0, n)
                    sg = nstg()
                    act(stg[:, sg, 0:n], ps[:, bg, 0:n], AF.Sigmoid, [PB[bg]], [STG[sg]])
                    if mode == "first":
                        tt(macc[:, e, c0:c0 + n], ps[:, bp, 0:n], stg[:, sg, 0:n], ALU.mult,
                           [PB[bp], STG[sg]], [M[e][si]])
                    else:
                        t = nstg()
                        tt(stg[:, t, 0:n], ps[:, bp, 0:n], stg[:, sg, 0:n], ALU.mult, [PB[bp], STG[sg]], [STG[t]])
                        if mode == "mid":
                            tt(macc[:, e, c0:c0 + n], macc[:, e, c0:c0 + n], stg[:, t, 0:n], ALU.add,
                               [M[e][si], STG[t]], [M[e][si]])
                        else:
                            tt(mrg(e, c0, n), macc[:, e, c0:c0 + n], stg[:, t, 0:n], ALU.add,
                               [M[e][si], STG[t]], [MR[e][si]], extra=[alias["last_sgu"]])
            rel(sG)
            rel(sW)

    def phase_g(l, subs, nxt_l):
        xb, X = cur["xb"], cur["X"]
        sl = [acq(), acq()]

        def grp(si, e2):
            c0, n = subs[si]
            bk = bank()
            slot = sl[e2 // 4]
            cc = e2 % 4
            lhs = [ring[:, slot, k, cc * 128:(cc + 1) * 128] for k in range(8)]
            rhs = [mrg(k, c0, n) for k in range(8)]
            alias["last_g"] = mm_group(bk, n, lhs, rhs, [RB[slot]] + [MR[k][si] for k in range(8)])
            tt(xb[:, e2, c0:c0 + n], ps[:, bk, 0:n], xb[:, e2, c0:c0 + n], ALU.add,
               [PB[bk], X[e2][si]], [X[e2][si]])

        if len(subs) == 1:
            for e2 in range(8):
                grp(0, e2)
            rel(sl[0])
            rel(sl[1])
            if nxt_l is not None:
                rmsnorm(nxt_l, 0, subs[0][0], subs[0][1], False)
            return
        for e2 in range(8):
            grp(0, e2)
        grp(1, 0)
        grp(1, 1)
        rr0 = rms_stats(0, subs[0][0], subs[0][1]) if nxt_l is not None else None
        for e2 in range(2, 8):
            grp(1, e2)
            if e2 == 3:
                rel(sl[0])
            if nxt_l is not None and e2 <= 5:
                for k in (2 * (e2 - 2), 2 * (e2 - 2) + 1):
                    rms_apply(nxt_l, 0, subs[0][0], subs[0][1], False, rr0, k)
        rel(sl[1])
        if nxt_l is not None:
            rmsnorm(nxt_l, 1, subs[1][0], subs[1][1], False)

    pairs = [(ti_, l_) for ti_ in range(len(tiles)) for l_ in range(n_layers)]
    load_small_dma(0)
    load_small_finish(0)
    pump()

    def tile_subs(nb_, lo=0):
        ss = [(lo * 128, (min(4, nb_) - lo) * 128)]
        if nb_ > 4:
            ss.append((512, (nb_ - 4) * 128))
        return ss

    LO_BY_D = {0: (2, 1), 1: (1, None), 2: (1, 0), 3: (0, None)}

    def halo_plan(ti_, l_):
        if ti_ != 0 or not TRIM:
            return 0, None
        lo_, so_ = LO_BY_D[min(n_layers - 1 - l_, 3)]
        return lo_, (None if so_ is None else so_ * 128 + 112)

    toff = [0]
    for nb_ in tiles:
        toff.append(toff[-1] + nb_ * 128)

    def load_x(ti_):
        ntk_ = tiles[ti_] * 128
        t0_ = toff[ti_]
        dst = bufs2[ti_ % 2]
        S.op("sp", lambda q: q.dma_start(out=dst[:, :, 0:ntk_], in_=xT_v[:, :, t0_:t0_ + ntk_]),
             writes=[grids2[ti_ % 2][k][si] for k in range(8) for si in range(2)], sem="x")

    load_x(0)
    set_tile(0)
    for si, (c0, n) in enumerate(tile_subs(tiles[0], 0)):
        rmsnorm(0, si, c0, n, False)
    ocol = 0
    for ti, nb in enumerate(tiles):
        set_tile(ti)
        ntk = nb * 128
        for l in range(n_layers):
            lo, so_col = halo_plan(ti, l)
            subs = tile_subs(nb, lo)
            pi = pairs.index((ti, l))
            nl = pairs[pi + 1][1] if pi + 1 < len(pairs) else None
            last = l == n_layers - 1
            prod_c(l, 0, subs, so_col)
            prod_b(l, 1, subs, ti, so_col)
            phase_f(l, 2, 0, subs, "first")
            prod_a(l, 0, subs, nb, lo)
            if nl is not None:
                load_small_dma(nl)
            phase_f(l, 1, 1, subs, "mid")
            phase_f(l, 0, 0, subs, "last")
            if last and ti + 1 < len(tiles):
                load_x(ti + 1)
            if nl is not None:
                load_small_finish(nl)
            phase_g(l, subs, None if last else l + 1)
        if ti + 1 < len(tiles):
            set_tile(ti + 1)
            for si, (c0, n) in enumerate(tile_subs(tiles[ti + 1])):
                rmsnorm(0, si, c0, n, False)
            set_tile(ti)
        skip = out_skip if ti == 0 else 0
        xb_, X_ = cur["xb"], cur["X"]
        for si, (c0, n) in enumerate(subs):
            if final_norm:
                rmsnorm(n_layers - 1, si, c0, n, True)
            a0 = max(c0, skip)
            a1 = c0 + n
            S.op("sp", lambda q, a0=a0, a1=a1, oc=ocol + a0 - skip, xb_=xb_: q.dma_start(
                out=outT_v[:, :, oc:oc + a1 - a0], in_=xb_[:, :, a0:a1]),
                reads=[X_[k][si] for k in range(8)], sem="out%d" % si)
        ocol += ntk - skip

    keys = sorted(S.cnt.keys())
    sems = {k: es.enter_context(nc.semaphore("s_" + k)) for k in keys}
    with nc.Block() as block:
        def runner(eng):
            def f(h):
                for fn, waits, key, inc in S.q[eng]:
                    for k, n in waits:
                        h.wait_ge(sems[k], n)
                    fn(h).then_inc(sems[key], inc)
                if eng == "sp":
                    for kk in keys:
                        if kk.startswith("out"):
                            h.wait_ge(sems[kk], S.cnt[kk])
            return f
        block.tensor(runner("pe"))
        block.scalar(runner("act"))
        block.vector(runner("dve"))
        block.gpsimd(runner("pool"))
        block.sync(runner("sp"))
    import os
    if os.environ.get("KDEBUG"):
        print("sbuf bytes remaining", nc.sbuf_bytes_remaining, {e: len(q) for e, q in S.q.items()})
    es.close()
    return nc


def _pack_small(p, layers):
    def vec(a):
        return np.asarray(a, np.float32).reshape(8, 128).T
    pvec = np.zeros((128, len(layers), NV, 8), np.float32)
    for i, l in enumerate(layers):
        pvec[:, i, V_NG] = vec(p["norm_g"][l])
        pvec[:, i, V_LNG] = vec(p["sgu_ln_g"][l])
        pvec[:, i, V_PB] = vec(p["pool_b"][l])
        pvec[:, i, V_PS] = vec(p["pool_scale"][l])
        pvec[:, i, V_CW0] = vec(p["conv_w"][l, 0])
        pvec[:, i, V_CW1] = vec(p["conv_w"][l, 1])
        pvec[:, i, V_CW2] = vec(p["conv_w"][l, 2])
        pvec[:, i, V_CB] = vec(p["conv_b"][l])
    ls = list(layers)
    sgu_wT = np.ascontiguousarray(
        np.asarray(p["sgu_w"], np.float32)[ls].transpose(0, 3, 1, 2)).reshape(len(ls), 128, 1024)
    rows = np.stack([np.asarray(p["sgu_ln_b"], np.float32)[ls],
                     np.asarray(p["sgu_b"], np.float32)[ls].reshape(len(ls), 1024)], axis=1)
    return {
        "pvec": np.ascontiguousarray(pvec.reshape(128, -1)),
        "fing": np.ascontiguousarray(vec(p["final_g"])),
        "sgu_wT": sgu_wT,
        "rows": np.ascontiguousarray(rows),
        "lng_bc": np.ascontiguousarray(np.broadcast_to(
            np.asarray(p["sgu_ln_g"], np.float32)[ls][:, None, :], (len(ls), 128, 1024))),
        "pool_w": np.ascontiguousarray(np.asarray(p["pool_w"], np.float32)[ls]),
        "w_in": np.ascontiguousarray(np.asarray(p["w_in"], np.float32)[ls]),
        "w_br": np.ascontiguousarray(np.stack([np.asarray(p["w_branch_a"], np.float32)[ls],
                                               np.asarray(p["w_branch_b"], np.float32)[ls],
                                               np.asarray(p["w_branch_c"], np.float32)[ls]], axis=1)),
        "w_out": np.ascontiguousarray(np.asarray(p["w_out"], np.float32)[ls]),
    }


def _invcnt(start):
    t = np.zeros((4, 16), np.float32)
    for g, w in enumerate((2, 4, 8, 16)):
        for i in range(16):
            t[g, i] = 1.0 / (min(i + 1, w) if start else w)
    return np.ascontiguousarray(np.broadcast_to(t.reshape(1, 64), (128, 64)))


def _shard_x(x, n_seg, own):
    B = x.shape[0]
    halo = HALO_BLK * 128
    outs = []
    for b in range(B):
        for s in range(n_seg):
            s0 = s * own
            a = np.zeros((D, halo + own), np.float32)
            a[:, halo:] = x[b, s0:s0 + own].T
            if s > 0:
                a[:, :halo] = x[b, s0 - halo:s0].T
            outs.append(a)
    return outs


def _run(x, p, layers, final_norm, tiles, n_seg, own):
    small = _pack_small(p, layers)
    nc = build(tiles, len(layers), final_norm, len(layers))
    xs = _shard_x(x, n_seg, own)
    in_maps = []
    for c, xc in enumerate(xs):
        m = dict(small)
        m["xT"] = xc
        m["invcnt"] = _invcnt(c % n_seg == 0)
        in_maps.append(m)
    res = run_bass_kernel_spmd(nc, in_maps, core_ids=list(range(len(xs))))
    B = x.shape[0]
    out = np.empty((B, n_seg * own, D), np.float32)
    for c, r in enumerate(res.results):
        b, s = divmod(c, n_seg)
        out[b, s * own:(s + 1) * own] = r["outT"].T
    return out


TILES = [7, 7, 7, 7, 6]


def kernel(**inputs):
    p = {k: np.asarray(v) for k, v in inputs.items()}
    x = np.asarray(p.pop("x"), np.float32)
    if FUSED:
        return _run(x, p, list(range(L)), True, TILES, 4, 4096)
    for l in range(L):
        x = _run(x, p, [l], l == L - 1, TILES, 4, 4096)
    return x
```

```python
from contextlib import ExitStack

import numpy as np
import concourse.bass as bass
import concourse.mybir as mybir
from concourse.bass_utils import run_bass_kernel_spmd

F32 = mybir.dt.float32
BF16 = mybir.dt.bfloat16
AF = mybir.ActivationFunctionType
ALU = mybir.AluOpType

D = 1024
L = 4
NIN = 12288
NV = 8
RMS_EPS = 1e-6
LN_EPS = 1e-5
NSLOT = 4
HALO_BLK = 2
TRIM = True
FUSED = True

BLK_AU, BLK_AV, BLK_AZ, BLK_BP, BLK_BZ, BLK_CH, BLK_CB, BLK_CC, BLK_CZ, BLK_G = 0, 2, 4, 6, 8, 10, 12, 14, 16, 18
V_NG, V_LNG, V_PB, V_PS, V_CW0, V_CW1, V_CW2, V_CB = range(8)


class Buf:
    __slots__ = ("w", "r")

    def __init__(self):
        self.w = None
        self.r = {}


class Sched:
    ENGS = ("pe", "act", "dve", "pool", "sp")

    def __init__(self):
        self.q = {e: [] for e in self.ENGS}
        self.cnt = {}
        self.seen = {e: {} for e in self.ENGS}

    def op(self, eng, fn, reads=(), writes=(), extra=(), sem=None):
        key = sem if sem is not None else eng
        inc = 16 if sem is not None else 1
        deps = {}

        def add(t):
            if t is None:
                return
            k, n = t
            if deps.get(k, 0) < n:
                deps[k] = n

        for b in reads:
            add(b.w)
        for b in writes:
            add(b.w)
            for k, n in b.r.items():
                add((k, n))
        for t in extra:
            add(t)
        waits = []
        seen = self.seen[eng]
        for k, n in deps.items():
            if eng == "pe" and k == "pe":
                continue
            if seen.get(k, 0) < n:
                waits.append((k, n))
                seen[k] = n
        self.cnt[key] = self.cnt.get(key, 0) + inc
        tok = (key, self.cnt[key])
        self.q[eng].append((fn, waits, key, inc))
        for b in reads:
            if b.r.get(key, 0) < tok[1]:
                b.r[key] = tok[1]
        for b in writes:
            b.w = tok
            b.r = {}
        return tok


def build(tiles, n_layers, final_norm, LW, fix_col=HALO_BLK * 128, out_skip=HALO_BLK * 128):
    nc = bass.Bass("TRN2", target_bir_lowering=False)
    NBLK = sum(tiles)
    NTOK = NBLK * 128
    TB = max(tiles)
    TT = TB * 128
    NOUT = NTOK - out_skip

    xT_d = nc.dram_tensor("xT", [D, NTOK], F32, kind="ExternalInput").ap()
    w_in_d = nc.dram_tensor("w_in", [LW, D, NIN], F32, kind="ExternalInput").ap()
    w_br_d = nc.dram_tensor("w_br", [LW, 3, D, D], F32, kind="ExternalInput").ap()
    w_out_d = nc.dram_tensor("w_out", [LW, D, D], F32, kind="ExternalInput").ap()
    pvec_d = nc.dram_tensor("pvec", [128, LW * NV * 8], F32, kind="ExternalInput").ap()
    fing_d = nc.dram_tensor("fing", [128, 8], F32, kind="ExternalInput").ap()
    sguT_d = nc.dram_tensor("sgu_wT", [LW, 128, 1024], F32, kind="ExternalInput").ap()
    rows_d = nc.dram_tensor("rows", [LW, 2, 1024], F32, kind="ExternalInput").ap()
    lngbc_d = nc.dram_tensor("lng_bc", [LW, 128, 1024], F32, kind="ExternalInput").ap()
    poolw_d = nc.dram_tensor("pool_w", [LW, 4, 256, 256], F32, kind="ExternalInput").ap()
    invc_d = nc.dram_tensor("invcnt", [128, 64], F32, kind="ExternalInput").ap()
    outT_d = nc.dram_tensor("outT", [D, NOUT], F32, kind="ExternalOutput").ap()

    xT_v = xT_d.rearrange("(k p) t -> p k t", p=128)
    outT_v = outT_d.rearrange("(k p) t -> p k t", p=128)

    S = Sched()
    es = ExitStack()

    def sb(name, shape, dt):
        return es.enter_context(nc.sbuf_tensor(name, shape, dt))

    NSTG = 8
    bufs2 = [sb("xa", [128, 8, TT], F32), sb("xc", [128, 8, TT], F32)]
    hT = sb("hT", [128, 8, TT], BF16)
    vm = sb("vm", [128, TB * 1024], BF16)
    yb = [sb("y0", [128, 8, TT], BF16), sb("y1", [128, 8, TT], BF16)]
    ring = sb("ring", [128, NSLOT, 8, 512], BF16)
    stg = sb("stg", [128, NSTG, 528], F32)
    stgb = sb("stgb", [128, 2, 512], BF16)
    gv = sb("gv", [128, 2, 1024], F32)
    gb = sb("gb", [128, 2, 516], F32)
    pbuf = sb("pbuf", [128, 3, 528], F32)
    dgrp = sb("dgrp", [128, 4, 2, 512], BF16)
    swT = sb("swT", [128, 1024], BF16)
    lngbc = sb("lngbc", [128, 1024], F32)
    pw = sb("pw", [128, 4, 2, 256], BF16)
    lb2 = sb("lb2", [2, 1024], BF16)
    rb2 = sb("rb2", [2, 1024], BF16)
    pv = sb("pv", [128, LW * NV * 8], F32)
    fg = sb("fg", [128, 8], F32)
    ic = sb("ic", [128, 64], F32)
    epsr = sb("epsr", [128, 2], F32)
    bsc = sb("bsc", [128, LW * 8], F32)
    onesb = sb("onesb", [128, 128], BF16)
    pst = sb("pst", [128, LW, 8, 16], F32)
    cst = sb("cst", [128, LW, 8, 2], F32)
    sm = sb("sm", [128, 4, 16], F32)
    ps = es.enter_context(nc.psum_tensor("ps", [128, 8, 512], F32))

    grids2 = [[[Buf() for _ in range(2)] for _ in range(8)] for _ in range(2)]
    cur = {}

    def set_tile(ti_):
        cur["xb"], cur["X"] = bufs2[ti_ % 2], grids2[ti_ % 2]
        cur["macc"], cur["M"] = bufs2[1 - ti_ % 2], grids2[1 - ti_ % 2]
    H = [[Buf() for _ in range(2)] for _ in range(8)]
    V = [Buf() for _ in range(TB)]
    MR = [[Buf() for _ in range(2)] for _ in range(8)]
    Y = [[[Buf() for _ in range(2)] for _ in range(8)] for _ in range(2)]
    RB = [Buf() for _ in range(NSLOT)]
    STG = [Buf() for _ in range(NSTG)]
    STGB = [Buf() for _ in range(2)]
    GV = [Buf() for _ in range(2)]
    GB = [Buf() for _ in range(2)]
    PBUF = [Buf() for _ in range(3)]
    PBUFH = [Buf() for _ in range(3)]
    DG = [[Buf() for _ in range(2)] for _ in range(4)]
    SWT, PW, LB, RB2, LNG = Buf(), Buf(), Buf(), Buf(), Buf()
    CONST = Buf()
    PST = [[Buf() for _ in range(8)] for _ in range(LW)]
    CST = [[Buf() for _ in range(8)] for _ in range(LW)]
    SM = [Buf() for _ in range(4)]
    PB = [Buf() for _ in range(8)]

    rot = {"stg": 0, "stgb": 0, "gv": 0, "gb": 0, "pbuf": 0, "dg": 0, "sm": 0, "pb": 0}

    def nxt(name, n):
        i = rot[name] % n
        rot[name] += 1
        return i

    def bank():
        return nxt("pb", 8)

    def nstg():
        return nxt("stg", NSTG)

    def pvcol(l, v, k):
        c = (l * NV + v) * 8 + k
        return pv[:, c:c + 1]

    def vtok(b):
        return vm[:, b * 1024:(b + 1) * 1024]

    def mrg(e, c0, n):
        return vm[:, e * TT + c0:e * TT + c0 + n]

    blocks = []

    def w_in_blk(l, c):
        return w_in_d[l].rearrange("(k p) n -> p k n", p=128)[:, :, c * 512:(c + 1) * 512]

    def w_br_blk(l, br, hq):
        return w_br_d[l, br].rearrange("(k p) n -> p k n", p=128)[:, :, hq * 512:(hq + 1) * 512]

    def w_out_blk(l, hq):
        return w_out_d[l].rearrange("(k p) n -> p k n", p=128)[:, :, hq * 512:(hq + 1) * 512]

    def layer_blocks(l):
        out = []
        for hq in range(2):
            out += [w_in_blk(l, BLK_CH + hq), w_in_blk(l, BLK_CC + hq),
                    w_in_blk(l, BLK_CZ + hq), w_in_blk(l, BLK_CB + hq)]
        for hq in range(2):
            out += [w_in_blk(l, BLK_BP + hq), w_in_blk(l, BLK_BZ + hq)]
        for hq in range(2):
            out += [w_in_blk(l, BLK_G + 4 + hq), w_br_blk(l, 2, hq)]
        out += [w_in_blk(l, BLK_AV), w_in_blk(l, BLK_AV + 1)]
        for hq in range(2):
            out += [w_in_blk(l, BLK_AU + hq), w_in_blk(l, BLK_AZ + hq)]
        for hq in range(2):
            out += [w_in_blk(l, BLK_G + 2 + hq), w_br_blk(l, 1, hq)]
        for hq in range(2):
            out += [w_in_blk(l, BLK_G + hq), w_br_blk(l, 0, hq)]
        out += [w_out_blk(l, 0), w_out_blk(l, 1)]
        return out

    for _ti in range(len(tiles)):
        for _l in range(n_layers):
            blocks.extend(layer_blocks(_l))
    ring_state = {"issued": 0, "acq": 0}

    def ring_issue():
        i = ring_state["issued"]
        if i >= len(blocks):
            return
        slot = i % NSLOT
        src = blocks[i]
        S.op("pool", lambda g, slot=slot, src=src: g.dma_start(out=ring[:, slot], in_=src),
             writes=[RB[slot]], sem="ring%d" % slot)
        ring_state["issued"] += 1

    def acquire():
        i = ring_state["acq"]
        while ring_state["issued"] <= i:
            ring_issue()
        ring_state["acq"] += 1
        return i % NSLOT

    live = set()

    def pump():
        while ring_state["issued"] < len(blocks):
            nslot = ring_state["issued"] % NSLOT
            if nslot in live or ring_state["issued"] - ring_state["acq"] >= NSLOT - len(live):
                break
            ring_issue()

    def acq():
        s_ = acquire()
        live.add(s_)
        return s_

    def rel(s_):
        live.discard(s_)
        pump()

    def mm_group(bk, n, lhs_list, rhs_list, reads, col0=0, prow=None):
        def fn(pe):
            ins = None
            last = len(lhs_list) - 1
            for idx in range(len(lhs_list)):
                o = ps[:, bk, col0:col0 + n] if prow is None else ps[0:prow, bk, col0:col0 + n]
                ins = pe.matmul(o, lhsT=lhs_list[idx], rhs=rhs_list[idx], start=(idx == 0), stop=(idx == last))
            return ins
        return S.op("pe", fn, reads=reads, writes=[PB[bk]])

    def proj_group(slot, cc, src, SRC, si, c0, n):
        bk = bank()
        lhs = [ring[:, slot, k, cc * 128:(cc + 1) * 128] for k in range(8)]
        rhs = [src[:, k, c0:c0 + n] for k in range(8)]
        mm_group(bk, n, lhs, rhs, [RB[slot]] + [SRC[k][si] for k in range(8)])
        return bk

    def act(out, in_, func, reads, writes, bias=None, scale=None):
        kw = {}
        if bias is not None:
            kw["bias"] = bias
        if scale is not None:
            kw["scale"] = scale
        return S.op("act", lambda a: a.activation(out=out, in_=in_, func=func, **kw), reads=reads, writes=writes)

    def tt(out, in0, in1, op, reads, writes, extra=()):
        return S.op("dve", lambda v: v.tensor_tensor(out=out, in0=in0, in1=in1, op=op), reads=reads, writes=writes,
                    extra=extra)

    def ptt(out, in0, in1, op, reads, writes, extra=()):
        return S.op("pool", lambda g: g.tensor_tensor(out=out, in0=in0, in1=in1, op=op), reads=reads, writes=writes,
                    extra=extra)

    def stt(out, in0, scalar, in1, op0, op1, reads, writes, extra=()):
        return S.op("dve", lambda v: v.scalar_tensor_tensor(out=out, in0=in0, scalar=scalar, in1=in1, op0=op0, op1=op1),
                    reads=reads, writes=writes, extra=extra)

    def ts(out, in0, s1, s2, op0, op1, reads, writes, extra=()):
        return S.op("dve", lambda v: v.tensor_scalar(out=out, in0=in0, scalar1=s1, scalar2=s2, op0=op0, op1=op1),
                    reads=reads, writes=writes, extra=extra)

    def cp(out, in_, reads, writes):
        return S.op("dve", lambda v: v.tensor_copy(out=out, in_=in_), reads=reads, writes=writes)

    S.op("sp", lambda q: q.dma_start(out=pv[:], in_=pvec_d[:, :]), writes=[CONST], sem="const")
    S.op("sp", lambda q: q.dma_start(out=fg[:], in_=fing_d[:, :]), writes=[CONST], sem="const")
    S.op("sp", lambda q: q.dma_start(out=ic[:], in_=invc_d[:, :]), writes=[CONST], sem="const")
    S.op("dve", lambda v: v.memset(onesb[:], 1.0), writes=[CONST])
    S.op("dve", lambda v: v.memset(lb2[:], 1.0), writes=[LB])
    S.op("dve", lambda v: v.memset(rb2[:], 0.0), writes=[RB2])
    S.op("dve", lambda v: v.memset(pst[:], 0.0), writes=[b for r in PST for b in r])
    S.op("dve", lambda v: v.memset(cst[:], 0.0), writes=[b for r in CST for b in r])
    for l in range(n_layers):
        pbc = (l * NV + V_PB) * 8
        psc = (l * NV + V_PS) * 8
        tt(bsc[:, l * 8:(l + 1) * 8], pv[:, pbc:pbc + 8], pv[:, psc:psc + 8], ALU.mult, [CONST], [CONST])
    S.op("dve", lambda v: v.memset(epsr[:, 0:1], RMS_EPS), writes=[CONST])
    S.op("dve", lambda v: v.memset(epsr[:, 1:2], LN_EPS), writes=[CONST])

    alias = {"last_sgu": None, "last_g": None}

    def load_small_dma(l):
        t = S.op("pool", lambda g: g.dma_start(out=swT[:], in_=sguT_d[l]), writes=[SWT], sem="sw")
        for g2 in range(4):
            t = S.op("pool", lambda g, g2=g2: g.dma_start(
                out=pw[:, g2], in_=poolw_d[l, g2].rearrange("(k p) d -> p k d", p=128)), writes=[PW], sem="sw")
        t = S.op("pool", lambda g: g.dma_start(out=lb2[0:1, :], in_=rows_d[l, 0:1, :]), writes=[LB], sem="sw")
        t = S.op("pool", lambda g: g.dma_start(out=rb2[1:2, :], in_=rows_d[l, 1:2, :]), writes=[RB2], sem="sw")
        t = S.op("pool", lambda g: g.dma_start(out=lngbc[:], in_=lngbc_d[l]), writes=[LNG], sem="sw")
        for b_ in (SWT, PW, LB, RB2, LNG):
            b_.w = t

    def load_small_finish(l):
        for g in range(8):
            S.op("dve", lambda v, g=g: v.memset(swT[64:128, g * 128:g * 128 + 64], 0.0), writes=[SWT])
        for half in range(2):
            bk = bank()

            def fn(pe, half=half, bk=bk):
                ins = None
                for gi in range(4):
                    g = half * 4 + gi
                    ins = pe.matmul(ps[0:1, bk, gi * 128:(gi + 1) * 128], lhsT=onesb[:, 0:1],
                                    rhs=swT[:, g * 128:(g + 1) * 128], start=True, stop=True)
                return ins
            S.op("pe", fn, reads=[SWT, CONST], writes=[PB[bk]])
            act(rb2[0:1, half * 512:(half + 1) * 512], ps[0:1, bk, 0:512], AF.Copy, [PB[bk]], [RB2])

    def rms_stats(si, c0, n):
        xb, X = cur["xb"], cur["X"]
        bk = bank()
        for k in range(8):
            sq = nxt("stgb", 2)
            act(stgb[:, sq, 0:n], xb[:, k, c0:c0 + n], AF.Square, [X[k][si]], [STGB[sq]])

            def fn(pe, k=k, sq=sq):
                return pe.matmul(ps[:, bk, 0:n], lhsT=onesb[:], rhs=stgb[:, sq, 0:n], start=(k == 0), stop=(k == 7))
            S.op("pe", fn, reads=[STGB[sq], CONST], writes=[PB[bk]])
        lnv = nstg()
        act(stg[:, lnv, 0:n], ps[:, bk, 0:n], AF.Ln, [PB[bk], CONST], [STG[lnv]], bias=epsr[:, 0:1], scale=1.0 / 1024.0)
        rr = nstg()
        act(stg[:, rr, 0:n], stg[:, lnv, 0:n], AF.Exp, [STG[lnv]], [STG[rr]], scale=-0.5)
        return rr

    def rms_apply(l, si, c0, n, final, rr, k):
        xb, X = cur["xb"], cur["X"]
        if final:
            stt(xb[:, k, c0:c0 + n], xb[:, k, c0:c0 + n], fg[:, k:k + 1], stg[:, rr, 0:n], ALU.mult, ALU.mult,
                [X[k][si], STG[rr], CONST], [X[k][si]])
        else:
            stt(hT[:, k, c0:c0 + n], xb[:, k, c0:c0 + n], pvcol(l, V_NG, k), stg[:, rr, 0:n],
                ALU.mult, ALU.mult, [X[k][si], STG[rr], CONST], [H[k][si]])

    def rmsnorm(l, si, c0, n, final):
        rr = rms_stats(si, c0, n)
        for k in range(8):
            rms_apply(l, si, c0, n, final, rr, k)

    def prod_c(l, yi, subs, so_col=None):
        macc, M = cur["macc"], cur["M"]
        for hq in range(2):
            sH = acq()
            sC = acq()
            if so_col is not None:
                for jj in range(4):
                    j = 4 * hq + jj
                    bh = proj_group(sH, jj, hT, H, 0, so_col, 16)
                    bc = proj_group(sC, jj, hT, H, 0, so_col, 16)
                    ch = nstg()
                    act(stg[:, ch, 0:16], ps[:, bh, 0:16], AF.Copy, [PB[bh]], [STG[ch]])
                    tt(cst[:, l, j, :], ps[:, bc, 14:16], stg[:, ch, 14:16], ALU.mult, [PB[bc], STG[ch]],
                       [CST[l][j]])
            for si, (c0, n) in enumerate(subs):
                for jj in range(4):
                    j = 4 * hq + jj
                    bh = proj_group(sH, jj, hT, H, si, c0, n)
                    bc = proj_group(sC, jj, hT, H, si, c0, n)
                    ch = nstg()
                    act(stg[:, ch, 0:n], ps[:, bh, 0:n], AF.Copy, [PB[bh]], [STG[ch]])
                    g_ = nxt("gb", 2)
                    cp(gb[:, g_, 0:2], cst[:, l, j, :], [CST[l][j]], [GB[g_]])
                    tt(gb[:, g_, 2:2 + n], ps[:, bc, 0:n], stg[:, ch, 0:n], ALU.mult, [PB[bc], STG[ch]], [GB[g_]])
                    cp(cst[:, l, j, :], gb[:, g_, n:n + 2], [GB[g_]], [CST[l][j]])
                    t0 = nstg()
                    act(stg[:, t0, 0:n], gb[:, g_, 2:2 + n], AF.Identity, [GB[g_], CONST], [STG[t0]],
                        bias=pvcol(l, V_CB, j), scale=pvcol(l, V_CW2, j))
                    t1 = nstg()
                    stt(stg[:, t1, 0:n], gb[:, g_, 1:1 + n], pvcol(l, V_CW1, j), stg[:, t0, 0:n], ALU.mult, ALU.add,
                        [GB[g_], STG[t0], CONST], [STG[t1]])
                    stt(macc[:, jj, c0:c0 + n], gb[:, g_, 0:n], pvcol(l, V_CW0, j), stg[:, t1, 0:n], ALU.mult, ALU.add,
                        [GB[g_], STG[t1], CONST], [M[jj][si]])
            rel(sH)
            rel(sC)
            sZ = acq()
            sB = acq()
            for si, (c0, n) in enumerate(subs):
                for jj in range(4):
                    j = 4 * hq + jj
                    bz = proj_group(sZ, jj, hT, H, si, c0, n)
                    bb = proj_group(sB, jj, hT, H, si, c0, n)
                    sz = nstg()
                    act(stg[:, sz, 0:n], ps[:, bz, 0:n], AF.Silu, [PB[bz]], [STG[sz]])
                    t3 = nstg()
                    tt(stg[:, t3, 0:n], ps[:, bb, 0:n], stg[:, sz, 0:n], ALU.mult, [PB[bb], STG[sz]], [STG[t3]])
                    tt(yb[yi][:, j, c0:c0 + n], stg[:, t3, 0:n], macc[:, jj, c0:c0 + n], ALU.mult,
                       [STG[t3], M[jj][si]], [Y[yi][j][si]])
            rel(sZ)
            rel(sB)

    def prod_b(l, yi, subs, ti, so_col=None):
        for hq in range(2):
            sP = acq()
            sZ = acq()
            if so_col is not None:
                for jj in range(4):
                    j = 4 * hq + jj
                    bk = proj_group(sP, jj, hT, H, 0, so_col, 16)
                    act(pst[:, l, j, :], ps[:, bk, 0:16], AF.Copy, [PB[bk]], [PST[l][j]])
            items = [(gg, si, c0, n) for gg in range(2) for si, (c0, n) in enumerate(subs)]
            dsets = {}

            def part1(gg, si, c0, n):
                g2 = 2 * hq + gg
                w = 2 ** (g2 + 1)
                dr = nxt("dg", 4)
                dsets[(gg, si)] = dr
                info = []
                for kc in range(2):
                    j = 2 * g2 + kc
                    jj = 2 * gg + kc
                    bk = proj_group(sP, jj, hT, H, si, c0, n)
                    pb_ = nxt("pbuf", 3)
                    info.append((kc, j, bk, pb_))
                for kc, j, bk, pb_ in info:
                    cp(pbuf[:, pb_, 0:16], pst[:, l, j, :], [PST[l][j]], [PBUFH[pb_]])
                for kc, j, bk, pb_ in info:
                    act(pbuf[:, pb_, 16:16 + n], ps[:, bk, 0:n], AF.Copy, [PB[bk]], [PBUF[pb_]])
                for kc, j, bk, pb_ in info:
                    cp(pst[:, l, j, :], pbuf[:, pb_, n:n + 16], [PBUF[pb_]], [PST[l][j]])
                fin = {}
                for kc, j, bk, pb_ in reversed(info):
                    P = pbuf[:, pb_]
                    addop = tt if kc == 0 else ptt
                    a_ = nstg()
                    A = stg[:, a_]
                    addop(A[:, 1:16 + n], P[:, 1:16 + n], P[:, 0:15 + n], ALU.add, [PBUF[pb_], PBUFH[pb_]], [STG[a_]])
                    cur_, curb = A, a_
                    sh, lo = 2, 3
                    while sh < w:
                        b_ = nstg()
                        Bv = stg[:, b_]
                        addop(Bv[:, lo:16 + n], cur_[:, lo:16 + n], cur_[:, lo - sh:16 + n - sh], ALU.add,
                              [STG[curb]], [STG[b_]])
                        cur_, curb = Bv, b_
                        sh *= 2
                        lo = 2 * sh - 1
                    fin[kc] = (cur_, curb)
                for kc, j, bk, pb_ in info:
                    P = pbuf[:, pb_]
                    cur_, curb = fin[kc]
                    stt(dgrp[:, dr, kc, 0:n], cur_[:, 16:16 + n], 1.0 / w, P[:, 16:16 + n], ALU.mult, ALU.subtract,
                        [STG[curb], PBUF[pb_]], [DG[dr][kc]])
                    if ti == 0 and c0 <= fix_col < c0 + n:
                        f0 = fix_col - c0
                        tf = nstg()
                        tt(stg[:, tf, 0:16], cur_[:, 16 + f0:32 + f0], ic[:, g2 * 16:(g2 + 1) * 16], ALU.mult,
                           [STG[curb], CONST], [STG[tf]])
                        tt(dgrp[:, dr, kc, f0:f0 + 16], stg[:, tf, 0:16], P[:, 16 + f0:32 + f0], ALU.subtract,
                           [STG[tf], PBUF[pb_]], [DG[dr][kc]])

            def part2(gg, si, c0, n):
                g2 = 2 * hq + gg
                dr = dsets[(gg, si)]
                for oc in range(2):
                    j = 2 * g2 + oc
                    jj = 2 * gg + oc
                    by = bank()
                    mm_group(by, n, [pw[:, g2, kc, oc * 128:(oc + 1) * 128] for kc in range(2)],
                             [dgrp[:, dr, kc, 0:n] for kc in range(2)], [PW, DG[dr][0], DG[dr][1]])
                    bz = proj_group(sZ, jj, hT, H, si, c0, n)
                    sz = nstg()
                    act(stg[:, sz, 0:n], ps[:, bz, 0:n], AF.Silu, [PB[bz]], [STG[sz]])
                    ya = nstg()
                    act(stg[:, ya, 0:n], ps[:, by, 0:n], AF.Identity, [PB[by], CONST], [STG[ya]],
                        bias=bsc[:, l * 8 + j:l * 8 + j + 1], scale=pvcol(l, V_PS, j))
                    tt(yb[yi][:, j, c0:c0 + n], stg[:, ya, 0:n], stg[:, sz, 0:n], ALU.mult,
                       [STG[ya], STG[sz]], [Y[yi][j][si]])

            SK = 2
            for idx, it in enumerate(items):
                part1(*it)
                if idx >= SK:
                    part2(*items[idx - SK])
            for it in items[max(0, len(items) - SK):]:
                part2(*it)
            rel(sP)
            rel(sZ)

    def prod_a(l, yi, subs, nb, blo=0):
        s0_ = acq()
        s1_ = acq()
        for bp0 in range(blo, nb, 2):
            grp = list(range(bp0, min(bp0 + 2, nb)))
            info = {}
            for b in grp:
                si = 0 if b < 4 else 1
                bA = bank()
                bB = bank()
                for half, (bk, sl) in enumerate(((bA, s0_), (bB, s1_))):
                    mm_group(bk, 512, [hT[:, k, b * 128:(b + 1) * 128] for k in range(8)],
                             [ring[:, sl, k, :] for k in range(8)], [RB[sl]] + [H[k][si] for k in range(8)])
                gi = nxt("gv", 2)
                m_ = nxt("sm", 4)
                info[b] = (gi, m_)
                act(gv[:, gi, 0:512], ps[:, bA, :], AF.Gelu_apprx_tanh, [PB[bA]], [GV[gi]])
                act(gv[:, gi, 512:1024], ps[:, bB, :], AF.Gelu_apprx_tanh, [PB[bB]], [GV[gi]])
                S.op("dve", lambda v, gi=gi, m_=m_: v.bn_stats(out=sm[:, m_, 0:6], in_=gv[:, gi, 0:512]),
                     reads=[GV[gi]], writes=[SM[m_]])
                S.op("dve", lambda v, gi=gi, m_=m_: v.bn_stats(out=sm[:, m_, 6:12], in_=gv[:, gi, 512:1024]),
                     reads=[GV[gi]], writes=[SM[m_]])
                S.op("dve", lambda v, m_=m_: v.bn_aggr(out=sm[:, m_, 12:14], in_=sm[:, m_, 0:12]),
                     reads=[SM[m_]], writes=[SM[m_]])
            for b in grp:
                gi, m_ = info[b]
                act(sm[:, m_, 14:15], sm[:, m_, 13:14], AF.Sqrt, [SM[m_], CONST], [SM[m_]], bias=epsr[:, 1:2])
            for b in grp:
                gi, m_ = info[b]
                S.op("dve", lambda v, m_=m_: v.reciprocal(out=sm[:, m_, 15:16], in_=sm[:, m_, 14:15]),
                     reads=[SM[m_]], writes=[SM[m_]])
                ts(gv[:, gi, :], gv[:, gi, :], sm[:, m_, 12:13], sm[:, m_, 15:16], ALU.subtract, ALU.mult,
                   [GV[gi], SM[m_]], [GV[gi]])
                tt(vtok(b), gv[:, gi, :], lngbc[:], ALU.mult, [GV[gi], LNG], [V[b]], extra=[alias["last_g"]])
        rel(s0_)
        rel(s1_)
        for hq in range(2):
            sU = acq()
            sZ = acq()
            for si, (c0, n) in enumerate(subs):
                for gg in range(4):
                    g = 4 * hq + gg
                    bm = bank()
                    b0 = c0 // 128
                    nbk = n // 128

                    def fn(pe, bm=bm, b0=b0, nbk=nbk, g=g):
                        ins = None
                        for bi in range(nbk):
                            o = ps[:, bm, bi * 128:(bi + 1) * 128]
                            pe.matmul(o, lhsT=vtok(b0 + bi)[:, g * 128:(g + 1) * 128], rhs=swT[:, g * 128:(g + 1) * 128],
                                      start=True, stop=False)
                            ins = pe.matmul(o, lhsT=lb2[0:2, g * 128:(g + 1) * 128],
                                            rhs=rb2[0:2, g * 128:(g + 1) * 128], start=False, stop=True)
                        return ins
                    alias["last_sgu"] = S.op("pe", fn, reads=[V[b0 + bi] for bi in range(nbk)] + [SWT, LB, RB2],
                                             writes=[PB[bm]])
                    bu = proj_group(sU, gg, hT, H, si, c0, n)
                    bz = proj_group(sZ, gg, hT, H, si, c0, n)
                    gu = nstg()
                    act(stg[:, gu, 0:n], ps[:, bu, 0:n], AF.Gelu_apprx_tanh, [PB[bu]], [STG[gu]])
                    th = nstg()
                    act(stg[:, th, 0:n], ps[:, bz, 0:n], AF.Tanh, [PB[bz]], [STG[th]], scale=0.5)
                    t = nstg()
                    tt(stg[:, t, 0:n], ps[:, bm, 0:n], stg[:, gu, 0:n], ALU.mult, [PB[bm], STG[gu]], [STG[t]])
                    t2 = nstg()
                    stt(stg[:, t2, 0:n], stg[:, th, 0:n], 1.0, stg[:, t, 0:n], ALU.add, ALU.mult,
                        [STG[th], STG[t]], [STG[t2]])
                    stt(yb[yi][:, g, c0:c0 + n], ps[:, bz, 0:n], 0.5, stg[:, t2, 0:n], ALU.mult, ALU.mult,
                        [PB[bz], STG[t2]], [Y[yi][g][si]])
            rel(sU)
            rel(sZ)

    def phase_f(l, br, yi, subs, mode):
        macc, M = cur["macc"], cur["M"]
        for hq in range(2):
            sG = acq()
            sW = acq()
            for ee in range(4):
                e = 4 * hq + ee
                for si, (c0, n) in enumerate(subs):
                    bg = proj_group(sG, ee, hT, H, si, c0, n)
                    bp = proj_group(sW, ee, yb[yi], Y[yi], si, c0, n)
                    sg = nstg()
                    act(stg[:, sg, 0:n], ps[:, bg, 0:n], AF.Sigmoid, [PB[bg]], [STG[sg]])
                    if mode == "first":
                        tt(macc[:, e, c0:c0 + n], ps[:, bp, 0:n], stg[:, sg, 0:n], ALU.mult,
                           [PB[bp], STG[sg]], [M[e][si]])
                    else:
                        t = nstg()
                        tt(stg[:, t, 0:n], ps[:, bp, 0:n], stg[:, sg, 0:n], ALU.mult, [PB[bp], STG[sg]], [STG[t]])
                        if mode == "mid":
                            tt(macc[:, e, c0:c0 + n], macc[:, e, c0:c0 + n], stg[:, t, 0:n], ALU.add,
                               [M[e][si], STG[t]], [M[e][si]])
                        else:
                            tt(mrg(e, c0, n), macc[:, e, c0:c0 + n], stg[:, t, 0:n], ALU.add,
                               [M[e][si], STG[t]], [MR[e][si]], extra=[alias["last_sgu"]])
            rel(sG)
            rel(sW)

    def phase_g(l, subs, nxt_l):
        xb, X = cur["xb"], cur["X"]
        sl = [acq(), acq()]

        def grp(si, e2):
            c0, n = subs[si]
            bk = bank()
            slot = sl[e2 // 4]
            cc = e2 % 4
            lhs = [ring[:, slot, k, cc * 128:(cc + 1) * 128] for k in range(8)]
            rhs = [mrg(k, c0, n) for k in range(8)]
            alias["last_g"] = mm_group(bk, n, lhs, rhs, [RB[slot]] + [MR[k][si] for k in range(8)])
            tt(xb[:, e2, c0:c0 + n], ps[:, bk, 0:n], xb[:, e2, c0:c0 + n], ALU.add,
               [PB[bk], X[e2][si]], [X[e2][si]])

        if len(subs) == 1:
            for e2 in range(8):
                grp(0, e2)
            rel(sl[0])
            rel(sl[1])
            if nxt_l is not None:
                rmsnorm(nxt_l, 0, subs[0][0], subs[0][1], False)
            return
        for e2 in range(8):
            grp(0, e2)
        grp(1, 0)
        grp(1, 1)
        rr0 = rms_stats(0, subs[0][0], subs[0][1]) if nxt_l is not None else None
        for e2 in range(2, 8):
            grp(1, e2)
            if e2 == 3:
                rel(sl[0])
            if nxt_l is not None and e2 <= 5:
                for k in (2 * (e2 - 2), 2 * (e2 - 2) + 1):
                    rms_apply(nxt_l, 0, subs[0][0], subs[0][1], False, rr0, k)
        rel(sl[1])
        if nxt_l is not None:
            rmsnorm(nxt_l, 1, subs[1][0], subs[1][1], False)

    pairs = [(ti_, l_) for ti_ in range(len(tiles)) for l_ in range(n_layers)]
    load_small_dma(0)
    load_small_finish(0)
    pump()

    def tile_subs(nb_, lo=0):
        ss = [(lo * 128, (min(4, nb_) - lo) * 128)]
        if nb_ > 4:
            ss.append((512, (nb_ - 4) * 128))
        return ss

    LO_BY_D = {0: (2, 1), 1: (1, None), 2: (1, 0), 3: (0, None)}

    def halo_plan(ti_, l_):
        if ti_ != 0 or not TRIM:
            return 0, None
        lo_, so_ = LO_BY_D[min(n_layers - 1 - l_, 3)]
        return lo_, (None if so_ is None else so_ * 128 + 112)

    toff = [0]
    for nb_ in tiles:
        toff.append(toff[-1] + nb_ * 128)

    def load_x(ti_):
        ntk_ = tiles[ti_] * 128
        t0_ = toff[ti_]
        dst = bufs2[ti_ % 2]
        S.op("sp", lambda q: q.dma_start(out=dst[:, :, 0:ntk_], in_=xT_v[:, :, t0_:t0_ + ntk_]),
             writes=[grids2[ti_ % 2][k][si] for k in range(8) for si in range(2)], sem="x")

    load_x(0)
    set_tile(0)
    for si, (c0, n) in enumerate(tile_subs(tiles[0], 0)):
        rmsnorm(0, si, c0, n, False)
    ocol = 0
    for ti, nb in enumerate(tiles):
        set_tile(ti)
        ntk = nb * 128
        for l in range(n_layers):
            lo, so_col = halo_plan(ti, l)
            subs = tile_subs(nb, lo)
            pi = pairs.index((ti, l))
            nl = pairs[pi + 1][1] if pi + 1 < len(pairs) else None
            last = l == n_layers - 1
            prod_c(l, 0, subs, so_col)
            prod_b(l, 1, subs, ti, so_col)
            phase_f(l, 2, 0, subs, "first")
            prod_a(l, 0, subs, nb, lo)
            if nl is not None:
                load_small_dma(nl)
            phase_f(l, 1, 1, subs, "mid")
            phase_f(l, 0, 0, subs, "last")
            if last and ti + 1 < len(tiles):
                load_x(ti + 1)
            if nl is not None:
                load_small_finish(nl)
            phase_g(l, subs, None if last else l + 1)
        if ti + 1 < len(tiles):
            set_tile(ti + 1)
            for si, (c0, n) in enumerate(tile_subs(tiles[ti + 1])):
                rmsnorm(0, si, c0, n, False)
            set_tile(ti)
        skip = out_skip if ti == 0 else 0
        xb_, X_ = cur["xb"], cur["X"]
        for si, (c0, n) in enumerate(subs):
            if final_norm:
                rmsnorm(n_layers - 1, si, c0, n, True)
            a0 = max(c0, skip)
            a1 = c0 + n
            S.op("sp", lambda q, a0=a0, a1=a1, oc=ocol + a0 - skip, xb_=xb_: q.dma_start(
                out=outT_v[:, :, oc:oc + a1 - a0], in_=xb_[:, :, a0:a1]),
                reads=[X_[k][si] for k in range(8)], sem="out%d" % si)
        ocol += ntk - skip

    keys = sorted(S.cnt.keys())
    sems = {k: es.enter_context(nc.semaphore("s_" + k)) for k in keys}
    with nc.Block() as block:
        def runner(eng):
            def f(h):
                for fn, waits, key, inc in S.q[eng]:
                    for k, n in waits:
                        h.wait_ge(sems[k], n)
                    fn(h).then_inc(sems[key], inc)
                if eng == "sp":
                    for kk in keys:
                        if kk.startswith("out"):
                            h.wait_ge(sems[kk], S.cnt[kk])
            return f
        block.tensor(runner("pe"))
        block.scalar(runner("act"))
        block.vector(runner("dve"))
        block.gpsimd(runner("pool"))
        block.sync(runner("sp"))
    import os
    if os.environ.get("KDEBUG"):
        print("sbuf bytes remaining", nc.sbuf_bytes_remaining, {e: len(q) for e, q in S.q.items()})
    es.close()
    return nc


def _pack_small(p, layers):
    def vec(a):
        return np.asarray(a, np.float32).reshape(8, 128).T
    pvec = np.zeros((128, len(layers), NV, 8), np.float32)
    for i, l in enumerate(layers):
        pvec[:, i, V_NG] = vec(p["norm_g"][l])
        pvec[:, i, V_LNG] = vec(p["sgu_ln_g"][l])
        pvec[:, i, V_PB] = vec(p["pool_b"][l])
        pvec[:, i, V_PS] = vec(p["pool_scale"][l])
        pvec[:, i, V_CW0] = vec(p["conv_w"][l, 0])
        pvec[:, i, V_CW1] = vec(p["conv_w"][l, 1])
        pvec[:, i, V_CW2] = vec(p["conv_w"][l, 2])
        pvec[:, i, V_CB] = vec(p["conv_b"][l])
    ls = list(layers)
    sgu_wT = np.ascontiguousarray(
        np.asarray(p["sgu_w"], np.float32)[ls].transpose(0, 3, 1, 2)).reshape(len(ls), 128, 1024)
    rows = np.stack([np.asarray(p["sgu_ln_b"], np.float32)[ls],
                     np.asarray(p["sgu_b"], np.float32)[ls].reshape(len(ls), 1024)], axis=1)
    return {
        "pvec": np.ascontiguousarray(pvec.reshape(128, -1)),
        "fing": np.ascontiguousarray(vec(p["final_g"])),
        "sgu_wT": sgu_wT,
        "rows": np.ascontiguousarray(rows),
        "lng_bc": np.ascontiguousarray(np.broadcast_to(
            np.asarray(p["sgu_ln_g"], np.float32)[ls][:, None, :], (len(ls), 128, 1024))),
        "pool_w": np.ascontiguousarray(np.asarray(p["pool_w"], np.float32)[ls]),
        "w_in": np.ascontiguousarray(np.asarray(p["w_in"], np.float32)[ls]),
        "w_br": np.ascontiguousarray(np.stack([np.asarray(p["w_branch_a"], np.float32)[ls],
                                               np.asarray(p["w_branch_b"], np.float32)[ls],
                                               np.asarray(p["w_branch_c"], np.float32)[ls]], axis=1)),
        "w_out": np.ascontiguousarray(np.asarray(p["w_out"], np.float32)[ls]),
    }


def _invcnt(start):
    t = np.zeros((4, 16), np.float32)
    for g, w in enumerate((2, 4, 8, 16)):
        for i in range(16):
            t[g, i] = 1.0 / (min(i + 1, w) if start else w)
    return np.ascontiguousarray(np.broadcast_to(t.reshape(1, 64), (128, 64)))


def _shard_x(x, n_seg, own):
    B = x.shape[0]
    halo = HALO_BLK * 128
    outs = []
    for b in range(B):
        for s in range(n_seg):
            s0 = s * own
            a = np.zeros((D, halo + own), np.float32)
            a[:, halo:] = x[b, s0:s0 + own].T
            if s > 0:
                a[:, :halo] = x[b, s0 - halo:s0].T
            outs.append(a)
    return outs


def _run(x, p, layers, final_norm, tiles, n_seg, own):
    small = _pack_small(p, layers)
    nc = build(tiles, len(layers), final_norm, len(layers))
    xs = _shard_x(x, n_seg, own)
    in_maps = []
    for c, xc in enumerate(xs):
        m = dict(small)
        m["xT"] = xc
        m["invcnt"] = _invcnt(c % n_seg == 0)
        in_maps.append(m)
    res = run_bass_kernel_spmd(nc, in_maps, core_ids=list(range(len(xs))))
    B = x.shape[0]
    out = np.empty((B, n_seg * own, D), np.float32)
    for c, r in enumerate(res.results):
        b, s = divmod(c, n_seg)
        out[b, s * own:(s + 1) * own] = r["outT"].T
    return out


TILES = [7, 7, 7, 7, 6]


def kernel(**inputs):
    p = {k: np.asarray(v) for k, v in inputs.items()}
    x = np.asarray(p.pop("x"), np.float32)
    if FUSED:
        return _run(x, p, list(range(L)), True, TILES, 4, 4096)
    for l in range(L):
        x = _run(x, p, [l], l == L - 1, TILES, 4, 4096)
    return x
```

```python
from contextlib import ExitStack

import numpy as np
import concourse.bass as bass
import concourse.mybir as mybir
from concourse.bass_utils import run_bass_kernel_spmd

F32 = mybir.dt.float32
BF16 = mybir.dt.bfloat16
AF = mybir.ActivationFunctionType
ALU = mybir.AluOpType

D = 1024
L = 4
NIN = 12288
NV = 8
RMS_EPS = 1e-6
LN_EPS = 1e-5
NSLOT = 4
HALO_BLK = 2
TRIM = True
FILL_N = 256
FUSED = True

BLK_AU, BLK_AV, BLK_AZ, BLK_BP, BLK_BZ, BLK_CH, BLK_CB, BLK_CC, BLK_CZ, BLK_G = 0, 2, 4, 6, 8, 10, 12, 14, 16, 18
V_NG, V_LNG, V_PB, V_PS, V_CW0, V_CW1, V_CW2, V_CB = range(8)


class Buf:
    __slots__ = ("w", "r")

    def __init__(self):
        self.w = None
        self.r = {}


class Sched:
    ENGS = ("pe", "act", "dve", "pool", "sp")

    def __init__(self):
        self.q = {e: [] for e in self.ENGS}
        self.cnt = {}
        self.seen = {e: {} for e in self.ENGS}

    def op(self, eng, fn, reads=(), writes=(), extra=(), sem=None):
        key = sem if sem is not None else eng
        inc = 16 if sem is not None else 1
        deps = {}

        def add(t):
            if t is None:
                return
            k, n = t
            if deps.get(k, 0) < n:
                deps[k] = n

        for b in reads:
            add(b.w)
        for b in writes:
            add(b.w)
            for k, n in b.r.items():
                add((k, n))
        for t in extra:
            add(t)
        waits = []
        seen = self.seen[eng]
        for k, n in deps.items():
            if eng == "pe" and k == "pe":
                continue
            if seen.get(k, 0) < n:
                waits.append((k, n))
                seen[k] = n
        self.cnt[key] = self.cnt.get(key, 0) + inc
        tok = (key, self.cnt[key])
        self.q[eng].append((fn, waits, key, inc))
        for b in reads:
            if b.r.get(key, 0) < tok[1]:
                b.r[key] = tok[1]
        for b in writes:
            b.w = tok
            b.r = {}
        return tok


def build(tiles, n_layers, final_norm, LW, fix_col=HALO_BLK * 128, out_skip=HALO_BLK * 128):
    nc = bass.Bass("TRN2", target_bir_lowering=False)
    NBLK = sum(tiles)
    NTOK = NBLK * 128
    TB = max(tiles)
    TT = TB * 128
    NOUT = NTOK - out_skip

    xT_d = nc.dram_tensor("xT", [D, NTOK], F32, kind="ExternalInput").ap()
    w_in_d = nc.dram_tensor("w_in", [LW, D, NIN], F32, kind="ExternalInput").ap()
    w_br_d = nc.dram_tensor("w_br", [LW, 3, D, D], F32, kind="ExternalInput").ap()
    w_out_d = nc.dram_tensor("w_out", [LW, D, D], F32, kind="ExternalInput").ap()
    pvec_d = nc.dram_tensor("pvec", [128, LW * NV * 8], F32, kind="ExternalInput").ap()
    fing_d = nc.dram_tensor("fing", [128, 8], F32, kind="ExternalInput").ap()
    sguT_d = nc.dram_tensor("sgu_wT", [LW, 128, 1024], F32, kind="ExternalInput").ap()
    rows_d = nc.dram_tensor("rows", [LW, 2, 1024], F32, kind="ExternalInput").ap()
    lngbc_d = nc.dram_tensor("lng_bc", [LW, 128, 1024], F32, kind="ExternalInput").ap()
    poolw_d = nc.dram_tensor("pool_w", [LW, 4, 256, 256], F32, kind="ExternalInput").ap()
    invc_d = nc.dram_tensor("invcnt", [128, 64], F32, kind="ExternalInput").ap()
    outT_d = nc.dram_tensor("outT", [D, NOUT], F32, kind="ExternalOutput").ap()

    xT_v = xT_d.rearrange("(k p) t -> p k t", p=128)
    outT_v = outT_d.rearrange("(k p) t -> p k t", p=128)

    S = Sched()
    es = ExitStack()

    def sb(name, shape, dt):
        return es.enter_context(nc.sbuf_tensor(name, shape, dt))

    NSTG = 8
    bufs2 = [sb("xa", [128, 8, TT], F32), sb("xc", [128, 8, TT], F32)]
    hT = sb("hT", [128, 8, TT], BF16)
    vm = sb("vm", [128, TB * 1024], BF16)
    yb = [sb("y0", [128, 8, TT], BF16), sb("y1", [128, 8, TT], BF16)]
    ring = sb("ring", [128, NSLOT, 8, 512], BF16)
    stg = sb("stg", [128, NSTG, 528], F32)
    stgb = sb("stgb", [128, 2, 512], BF16)
    gv = sb("gv", [128, 2, 1024], F32)
    gb = sb("gb", [128, 2, 516], F32)
    pbuf = sb("pbuf", [128, 3, 528], F32)
    dgrp = sb("dgrp", [128, 4, 2, 512], BF16)
    swT = sb("swT", [128, 1024], BF16)
    lngbc = sb("lngbc", [128, 1024], F32)
    pw = sb("pw", [128, 4, 2, 256], BF16)
    lb2 = sb("lb2", [2, 1024], BF16)
    rb2 = sb("rb2", [2, 1024], BF16)
    pv = sb("pv", [128, LW * NV * 8], F32)
    fg = sb("fg", [128, 8], F32)
    ic = sb("ic", [128, 64], F32)
    epsr = sb("epsr", [128, 2], F32)
    bsc = sb("bsc", [128, LW * 8], F32)
    onesb = sb("onesb", [128, 128], BF16)
    zer = sb("zer", [128, 256], BF16)
    pst = sb("pst", [128, LW, 8, 16], F32)
    cst = sb("cst", [128, LW, 8, 2], F32)
    sm = sb("sm", [128, 4, 16], F32)
    ps = es.enter_context(nc.psum_tensor("ps", [128, 8, 512], F32))

    grids2 = [[[Buf() for _ in range(2)] for _ in range(8)] for _ in range(2)]
    cur = {}

    def set_tile(ti_):
        cur["xb"], cur["X"] = bufs2[ti_ % 2], grids2[ti_ % 2]
        cur["macc"], cur["M"] = bufs2[1 - ti_ % 2], grids2[1 - ti_ % 2]
    H = [[Buf() for _ in range(2)] for _ in range(8)]
    V = [Buf() for _ in range(TB)]
    MR = [[Buf() for _ in range(2)] for _ in range(8)]
    Y = [[[Buf() for _ in range(2)] for _ in range(8)] for _ in range(2)]
    RB = [Buf() for _ in range(NSLOT)]
    STG = [Buf() for _ in range(NSTG)]
    STGB = [Buf() for _ in range(2)]
    GV = [Buf() for _ in range(2)]
    GB = [Buf() for _ in range(2)]
    PBUF = [Buf() for _ in range(3)]
    PBUFH = [Buf() for _ in range(3)]
    DG = [[Buf() for _ in range(2)] for _ in range(4)]
    SWT, PW, LB, RB2, LNG = Buf(), Buf(), Buf(), Buf(), Buf()
    CONST = Buf()
    PST = [[Buf() for _ in range(8)] for _ in range(LW)]
    CST = [[Buf() for _ in range(8)] for _ in range(LW)]
    SM = [Buf() for _ in range(4)]
    PB = [Buf() for _ in range(8)]

    rot = {"stg": 0, "stgb": 0, "gv": 0, "gb": 0, "pbuf": 0, "dg": 0, "sm": 0, "pb": 0}

    def nxt(name, n):
        i = rot[name] % n
        rot[name] += 1
        return i

    def bank():
        return nxt("pb", 8)

    def nstg():
        return nxt("stg", NSTG)

    def pvcol(l, v, k):
        c = (l * NV + v) * 8 + k
        return pv[:, c:c + 1]

    def vtok(b):
        return vm[:, b * 1024:(b + 1) * 1024]

    def mrg(e, c0, n):
        return vm[:, e * TT + c0:e * TT + c0 + n]

    blocks = []

    def w_in_blk(l, c):
        return w_in_d[l].rearrange("(k p) n -> p k n", p=128)[:, :, c * 512:(c + 1) * 512]

    def w_br_blk(l, br, hq):
        return w_br_d[l, br].rearrange("(k p) n -> p k n", p=128)[:, :, hq * 512:(hq + 1) * 512]

    def w_out_blk(l, hq):
        return w_out_d[l].rearrange("(k p) n -> p k n", p=128)[:, :, hq * 512:(hq + 1) * 512]

    def layer_blocks(l):
        out = []
        for hq in range(2):
            out += [w_in_blk(l, BLK_CH + hq), w_in_blk(l, BLK_CC + hq),
                    w_in_blk(l, BLK_CZ + hq), w_in_blk(l, BLK_CB + hq)]
        for hq in range(2):
            out += [w_in_blk(l, BLK_BP + hq), w_in_blk(l, BLK_BZ + hq)]
        for hq in range(2):
            out += [w_in_blk(l, BLK_G + 4 + hq), w_br_blk(l, 2, hq)]
        out += [w_in_blk(l, BLK_AV), w_in_blk(l, BLK_AV + 1)]
        for hq in range(2):
            out += [w_in_blk(l, BLK_AU + hq), w_in_blk(l, BLK_AZ + hq)]
        for hq in range(2):
            out += [w_in_blk(l, BLK_G + 2 + hq), w_br_blk(l, 1, hq)]
        for hq in range(2):
            out += [w_in_blk(l, BLK_G + hq), w_br_blk(l, 0, hq)]
        out += [w_out_blk(l, 0), w_out_blk(l, 1)]
        return out

    for _ti in range(len(tiles)):
        for _l in range(n_layers):
            blocks.extend(layer_blocks(_l))
    ring_state = {"issued": 0, "acq": 0}

    def ring_issue():
        i = ring_state["issued"]
        if i >= len(blocks):
            return
        slot = i % NSLOT
        src = blocks[i]
        S.op("pool", lambda g, slot=slot, src=src: g.dma_start(out=ring[:, slot], in_=src),
             writes=[RB[slot]], sem="ring%d" % slot)
        ring_state["issued"] += 1

    def acquire():
        i = ring_state["acq"]
        while ring_state["issued"] <= i:
            ring_issue()
        ring_state["acq"] += 1
        return i % NSLOT

    live = set()

    def pump():
        while ring_state["issued"] < len(blocks):
            nslot = ring_state["issued"] % NSLOT
            if nslot in live or ring_state["issued"] - ring_state["acq"] >= NSLOT - len(live):
                break
            ring_issue()

    def acq():
        s_ = acquire()
        live.add(s_)
        return s_

    def rel(s_):
        live.discard(s_)
        pump()

    def mm_group(bk, n, lhs_list, rhs_list, reads, col0=0, prow=None, fill=0):
        def fn(pe):
            ins = None
            last = len(lhs_list) - 1
            if fill:
                pe.matmul(ps[:, bk, 0:fill], lhsT=zer[:, 0:128], rhs=zer[:, 0:fill], start=True, stop=True)
            for idx in range(len(lhs_list)):
                o = ps[:, bk, col0:col0 + n] if prow is None else ps[0:prow, bk, col0:col0 + n]
                ins = pe.matmul(o, lhsT=lhs_list[idx], rhs=rhs_list[idx], start=(idx == 0), stop=(idx == last))
            return ins
        return S.op("pe", fn, reads=reads, writes=[PB[bk]])

    def proj_group(slot, cc, src, SRC, si, c0, n):
        bk = bank()
        lhs = [ring[:, slot, k, cc * 128:(cc + 1) * 128] for k in range(8)]
        rhs = [src[:, k, c0:c0 + n] for k in range(8)]
        mm_group(bk, n, lhs, rhs, [RB[slot], CONST] + [SRC[k][si] for k in range(8)],
                 fill=(min(FILL_N, n) if n >= 256 else 0))
        return bk

    def act(out, in_, func, reads, writes, bias=None, scale=None):
        kw = {}
        if bias is not None:
            kw["bias"] = bias
        if scale is not None:
            kw["scale"] = scale
        return S.op("act", lambda a: a.activation(out=out, in_=in_, func=func, **kw), reads=reads, writes=writes)

    def tt(out, in0, in1, op, reads, writes, extra=()):
        return S.op("dve", lambda v: v.tensor_tensor(out=out, in0=in0, in1=in1, op=op), reads=reads, writes=writes,
                    extra=extra)

    def ptt(out, in0, in1, op, reads, writes, extra=()):
        return S.op("pool", lambda g: g.tensor_tensor(out=out, in0=in0, in1=in1, op=op), reads=reads, writes=writes,
                    extra=extra)

    def stt(out, in0, scalar, in1, op0, op1, reads, writes, extra=()):
        return S.op("dve", lambda v: v.scalar_tensor_tensor(out=out, in0=in0, scalar=scalar, in1=in1, op0=op0, op1=op1),
                    reads=reads, writes=writes, extra=extra)

    def ts(out, in0, s1, s2, op0, op1, reads, writes, extra=()):
        return S.op("dve", lambda v: v.tensor_scalar(out=out, in0=in0, scalar1=s1, scalar2=s2, op0=op0, op1=op1),
                    reads=reads, writes=writes, extra=extra)

    def cp(out, in_, reads, writes):
        return S.op("dve", lambda v: v.tensor_copy(out=out, in_=in_), reads=reads, writes=writes)

    S.op("sp", lambda q: q.dma_start(out=pv[:], in_=pvec_d[:, :]), writes=[CONST], sem="const")
    S.op("sp", lambda q: q.dma_start(out=fg[:], in_=fing_d[:, :]), writes=[CONST], sem="const")
    S.op("sp", lambda q: q.dma_start(out=ic[:], in_=invc_d[:, :]), writes=[CONST], sem="const")
    S.op("dve", lambda v: v.memset(onesb[:], 1.0), writes=[CONST])
    S.op("dve", lambda v: v.memset(zer[:], 0.0), writes=[CONST])
    S.op("dve", lambda v: v.memset(lb2[:], 1.0), writes=[LB])
    S.op("dve", lambda v: v.memset(rb2[:], 0.0), writes=[RB2])
    S.op("dve", lambda v: v.memset(pst[:], 0.0), writes=[b for r in PST for b in r])
    S.op("dve", lambda v: v.memset(cst[:], 0.0), writes=[b for r in CST for b in r])
    for l in range(n_layers):
        pbc = (l * NV + V_PB) * 8
        psc = (l * NV + V_PS) * 8
        tt(bsc[:, l * 8:(l + 1) * 8], pv[:, pbc:pbc + 8], pv[:, psc:psc + 8], ALU.mult, [CONST], [CONST])
    S.op("dve", lambda v: v.memset(epsr[:, 0:1], RMS_EPS), writes=[CONST])
    S.op("dve", lambda v: v.memset(epsr[:, 1:2], LN_EPS), writes=[CONST])

    alias = {"last_sgu": None, "last_g": None}

    def load_small_dma(l):
        t = S.op("pool", lambda g: g.dma_start(out=swT[:], in_=sguT_d[l]), writes=[SWT], sem="sw")
        for g2 in range(4):
            t = S.op("pool", lambda g, g2=g2: g.dma_start(
                out=pw[:, g2], in_=poolw_d[l, g2].rearrange("(k p) d -> p k d", p=128)), writes=[PW], sem="sw")
        t = S.op("pool", lambda g: g.dma_start(out=lb2[0:1, :], in_=rows_d[l, 0:1, :]), writes=[LB], sem="sw")
        t = S.op("pool", lambda g: g.dma_start(out=rb2[1:2, :], in_=rows_d[l, 1:2, :]), writes=[RB2], sem="sw")
        t = S.op("pool", lambda g: g.dma_start(out=lngbc[:], in_=lngbc_d[l]), writes=[LNG], sem="sw")
        for b_ in (SWT, PW, LB, RB2, LNG):
            b_.w = t

    def load_small_finish(l):
        for g in range(8):
            S.op("dve", lambda v, g=g: v.memset(swT[64:128, g * 128:g * 128 + 64], 0.0), writes=[SWT])
        for half in range(2):
            bk = bank()

            def fn(pe, half=half, bk=bk):
                ins = None
                for gi in range(4):
                    g = half * 4 + gi
                    ins = pe.matmul(ps[0:1, bk, gi * 128:(gi + 1) * 128], lhsT=onesb[:, 0:1],
                                    rhs=swT[:, g * 128:(g + 1) * 128], start=True, stop=True)
                return ins
            S.op("pe", fn, reads=[SWT, CONST], writes=[PB[bk]])
            act(rb2[0:1, half * 512:(half + 1) * 512], ps[0:1, bk, 0:512], AF.Copy, [PB[bk]], [RB2])

    def rms_stats(si, c0, n):
        xb, X = cur["xb"], cur["X"]
        bk = bank()
        for k in range(8):
            sq = nxt("stgb", 2)
            act(stgb[:, sq, 0:n], xb[:, k, c0:c0 + n], AF.Square, [X[k][si]], [STGB[sq]])

            def fn(pe, k=k, sq=sq):
                return pe.matmul(ps[:, bk, 0:n], lhsT=onesb[:], rhs=stgb[:, sq, 0:n], start=(k == 0), stop=(k == 7))
            S.op("pe", fn, reads=[STGB[sq], CONST], writes=[PB[bk]])
        lnv = nstg()
        act(stg[:, lnv, 0:n], ps[:, bk, 0:n], AF.Ln, [PB[bk], CONST], [STG[lnv]], bias=epsr[:, 0:1], scale=1.0 / 1024.0)
        rr = nstg()
        act(stg[:, rr, 0:n], stg[:, lnv, 0:n], AF.Exp, [STG[lnv]], [STG[rr]], scale=-0.5)
        return rr

    def rms_apply(l, si, c0, n, final, rr, k):
        xb, X = cur["xb"], cur["X"]
        if final:
            stt(xb[:, k, c0:c0 + n], xb[:, k, c0:c0 + n], fg[:, k:k + 1], stg[:, rr, 0:n], ALU.mult, ALU.mult,
                [X[k][si], STG[rr], CONST], [X[k][si]])
        else:
            stt(hT[:, k, c0:c0 + n], xb[:, k, c0:c0 + n], pvcol(l, V_NG, k), stg[:, rr, 0:n],
                ALU.mult, ALU.mult, [X[k][si], STG[rr], CONST], [H[k][si]])

    def rmsnorm(l, si, c0, n, final):
        rr = rms_stats(si, c0, n)
        for k in range(8):
            rms_apply(l, si, c0, n, final, rr, k)

    def prod_c(l, yi, subs, so_col=None):
        macc, M = cur["macc"], cur["M"]
        for hq in range(2):
            sH = acq()
            sC = acq()
            if so_col is not None:
                for jj in range(4):
                    j = 4 * hq + jj
                    bh = proj_group(sH, jj, hT, H, 0, so_col, 16)
                    bc = proj_group(sC, jj, hT, H, 0, so_col, 16)
                    ch = nstg()
                    act(stg[:, ch, 0:16], ps[:, bh, 0:16], AF.Copy, [PB[bh]], [STG[ch]])
                    tt(cst[:, l, j, :], ps[:, bc, 14:16], stg[:, ch, 14:16], ALU.mult, [PB[bc], STG[ch]],
                       [CST[l][j]])
            for si, (c0, n) in enumerate(subs):
                for jj in range(4):
                    j = 4 * hq + jj
                    bh = proj_group(sH, jj, hT, H, si, c0, n)
                    bc = proj_group(sC, jj, hT, H, si, c0, n)
                    ch = nstg()
                    act(stg[:, ch, 0:n], ps[:, bh, 0:n], AF.Copy, [PB[bh]], [STG[ch]])
                    g_ = nxt("gb", 2)
                    cp(gb[:, g_, 0:2], cst[:, l, j, :], [CST[l][j]], [GB[g_]])
                    tt(gb[:, g_, 2:2 + n], ps[:, bc, 0:n], stg[:, ch, 0:n], ALU.mult, [PB[bc], STG[ch]], [GB[g_]])
                    cp(cst[:, l, j, :], gb[:, g_, n:n + 2], [GB[g_]], [CST[l][j]])
                    t0 = nstg()
                    act(stg[:, t0, 0:n], gb[:, g_, 2:2 + n], AF.Identity, [GB[g_], CONST], [STG[t0]],
                        bias=pvcol(l, V_CB, j), scale=pvcol(l, V_CW2, j))
                    t1 = nstg()
                    stt(stg[:, t1, 0:n], gb[:, g_, 1:1 + n], pvcol(l, V_CW1, j), stg[:, t0, 0:n], ALU.mult, ALU.add,
                        [GB[g_], STG[t0], CONST], [STG[t1]])
                    stt(macc[:, jj, c0:c0 + n], gb[:, g_, 0:n], pvcol(l, V_CW0, j), stg[:, t1, 0:n], ALU.mult, ALU.add,
                        [GB[g_], STG[t1], CONST], [M[jj][si]])
            rel(sH)
            rel(sC)
            sZ = acq()
            sB = acq()
            for si, (c0, n) in enumerate(subs):
                for jj in range(4):
                    j = 4 * hq + jj
                    bz = proj_group(sZ, jj, hT, H, si, c0, n)
                    bb = proj_group(sB, jj, hT, H, si, c0, n)
                    sz = nstg()
                    act(stg[:, sz, 0:n], ps[:, bz, 0:n], AF.Silu, [PB[bz]], [STG[sz]])
                    t3 = nstg()
                    tt(stg[:, t3, 0:n], ps[:, bb, 0:n], stg[:, sz, 0:n], ALU.mult, [PB[bb], STG[sz]], [STG[t3]])
                    tt(yb[yi][:, j, c0:c0 + n], stg[:, t3, 0:n], macc[:, jj, c0:c0 + n], ALU.mult,
                       [STG[t3], M[jj][si]], [Y[yi][j][si]])
            rel(sZ)
            rel(sB)

    def prod_b(l, yi, subs, ti, so_col=None):
        for hq in range(2):
            sP = acq()
            sZ = acq()
            if so_col is not None:
                for jj in range(4):
                    j = 4 * hq + jj
                    bk = proj_group(sP, jj, hT, H, 0, so_col, 16)
                    act(pst[:, l, j, :], ps[:, bk, 0:16], AF.Copy, [PB[bk]], [PST[l][j]])
            items = [(gg, si, c0, n) for gg in range(2) for si, (c0, n) in enumerate(subs)]
            dsets = {}

            def part1(gg, si, c0, n):
                g2 = 2 * hq + gg
                w = 2 ** (g2 + 1)
                dr = nxt("dg", 4)
                dsets[(gg, si)] = dr
                info = []
                for kc in range(2):
                    j = 2 * g2 + kc
                    jj = 2 * gg + kc
                    bk = proj_group(sP, jj, hT, H, si, c0, n)
                    pb_ = nxt("pbuf", 3)
                    info.append((kc, j, bk, pb_))
                for kc, j, bk, pb_ in info:
                    cp(pbuf[:, pb_, 0:16], pst[:, l, j, :], [PST[l][j]], [PBUFH[pb_]])
                for kc, j, bk, pb_ in info:
                    act(pbuf[:, pb_, 16:16 + n], ps[:, bk, 0:n], AF.Copy, [PB[bk]], [PBUF[pb_]])
                for kc, j, bk, pb_ in info:
                    cp(pst[:, l, j, :], pbuf[:, pb_, n:n + 16], [PBUF[pb_]], [PST[l][j]])
                fin = {}
                for kc, j, bk, pb_ in reversed(info):
                    P = pbuf[:, pb_]
                    addop = tt if kc == 0 else ptt
                    a_ = nstg()
                    A = stg[:, a_]
                    addop(A[:, 1:16 + n], P[:, 1:16 + n], P[:, 0:15 + n], ALU.add, [PBUF[pb_], PBUFH[pb_]], [STG[a_]])
                    cur_, curb = A, a_
                    sh, lo = 2, 3
                    while sh < w:
                        b_ = nstg()
                        Bv = stg[:, b_]
                        addop(Bv[:, lo:16 + n], cur_[:, lo:16 + n], cur_[:, lo - sh:16 + n - sh], ALU.add,
                              [STG[curb]], [STG[b_]])
                        cur_, curb = Bv, b_
                        sh *= 2
                        lo = 2 * sh - 1
                    fin[kc] = (cur_, curb)
                for kc, j, bk, pb_ in info:
                    P = pbuf[:, pb_]
                    cur_, curb = fin[kc]
                    stt(dgrp[:, dr, kc, 0:n], cur_[:, 16:16 + n], 1.0 / w, P[:, 16:16 + n], ALU.mult, ALU.subtract,
                        [STG[curb], PBUF[pb_]], [DG[dr][kc]])
                    if ti == 0 and c0 <= fix_col < c0 + n:
                        f0 = fix_col - c0
                        tf = nstg()
                        tt(stg[:, tf, 0:16], cur_[:, 16 + f0:32 + f0], ic[:, g2 * 16:(g2 + 1) * 16], ALU.mult,
                           [STG[curb], CONST], [STG[tf]])
                        tt(dgrp[:, dr, kc, f0:f0 + 16], stg[:, tf, 0:16], P[:, 16 + f0:32 + f0], ALU.subtract,
                           [STG[tf], PBUF[pb_]], [DG[dr][kc]])

            def part2(gg, si, c0, n):
                g2 = 2 * hq + gg
                dr = dsets[(gg, si)]
                for oc in range(2):
                    j = 2 * g2 + oc
                    jj = 2 * gg + oc
                    by = bank()
                    mm_group(by, n, [pw[:, g2, kc, oc * 128:(oc + 1) * 128] for kc in range(2)],
                             [dgrp[:, dr, kc, 0:n] for kc in range(2)], [PW, DG[dr][0], DG[dr][1]])
                    bz = proj_group(sZ, jj, hT, H, si, c0, n)
                    sz = nstg()
                    act(stg[:, sz, 0:n], ps[:, bz, 0:n], AF.Silu, [PB[bz]], [STG[sz]])
                    ya = nstg()
                    act(stg[:, ya, 0:n], ps[:, by, 0:n], AF.Identity, [PB[by], CONST], [STG[ya]],
                        bias=bsc[:, l * 8 + j:l * 8 + j + 1], scale=pvcol(l, V_PS, j))
                    tt(yb[yi][:, j, c0:c0 + n], stg[:, ya, 0:n], stg[:, sz, 0:n], ALU.mult,
                       [STG[ya], STG[sz]], [Y[yi][j][si]])

            SK = 2
            for idx, it in enumerate(items):
                part1(*it)
                if idx >= SK:
                    part2(*items[idx - SK])
            for it in items[max(0, len(items) - SK):]:
                part2(*it)
            rel(sP)
            rel(sZ)

    def prod_a(l, yi, subs, nb, blo=0):
        s0_ = acq()
        s1_ = acq()
        for bp0 in range(blo, nb, 2):
            grp = list(range(bp0, min(bp0 + 2, nb)))
            info = {}
            for b in grp:
                si = 0 if b < 4 else 1
                bA = bank()
                bB = bank()
                for half, (bk, sl) in enumerate(((bA, s0_), (bB, s1_))):
                    mm_group(bk, 512, [hT[:, k, b * 128:(b + 1) * 128] for k in range(8)],
                             [ring[:, sl, k, :] for k in range(8)], [RB[sl]] + [H[k][si] for k in range(8)])
                gi = nxt("gv", 2)
                m_ = nxt("sm", 4)
                info[b] = (gi, m_)
                act(gv[:, gi, 0:512], ps[:, bA, :], AF.Gelu_apprx_tanh, [PB[bA]], [GV[gi]])
                act(gv[:, gi, 512:1024], ps[:, bB, :], AF.Gelu_apprx_tanh, [PB[bB]], [GV[gi]])
                S.op("dve", lambda v, gi=gi, m_=m_: v.bn_stats(out=sm[:, m_, 0:6], in_=gv[:, gi, 0:512]),
                     reads=[GV[gi]], writes=[SM[m_]])
                S.op("dve", lambda v, gi=gi, m_=m_: v.bn_stats(out=sm[:, m_, 6:12], in_=gv[:, gi, 512:1024]),
                     reads=[GV[gi]], writes=[SM[m_]])
                S.op("dve", lambda v, m_=m_: v.bn_aggr(out=sm[:, m_, 12:14], in_=sm[:, m_, 0:12]),
                     reads=[SM[m_]], writes=[SM[m_]])
            for b in grp:
                gi, m_ = info[b]
                act(sm[:, m_, 14:15], sm[:, m_, 13:14], AF.Sqrt, [SM[m_], CONST], [SM[m_]], bias=epsr[:, 1:2])
            for b in grp:
                gi, m_ = info[b]
                S.op("dve", lambda v, m_=m_: v.reciprocal(out=sm[:, m_, 15:16], in_=sm[:, m_, 14:15]),
                     reads=[SM[m_]], writes=[SM[m_]])
                ts(gv[:, gi, :], gv[:, gi, :], sm[:, m_, 12:13], sm[:, m_, 15:16], ALU.subtract, ALU.mult,
                   [GV[gi], SM[m_]], [GV[gi]])
                tt(vtok(b), gv[:, gi, :], lngbc[:], ALU.mult, [GV[gi], LNG], [V[b]], extra=[alias["last_g"]])
        rel(s0_)
        rel(s1_)
        for hq in range(2):
            sU = acq()
            sZ = acq()
            for si, (c0, n) in enumerate(subs):
                for gg in range(4):
                    g = 4 * hq + gg
                    bm = bank()
                    b0 = c0 // 128
                    nbk = n // 128

                    def fn(pe, bm=bm, b0=b0, nbk=nbk, g=g):
                        ins = None
                        for bi in range(nbk):
                            o = ps[:, bm, bi * 128:(bi + 1) * 128]
                            pe.matmul(o, lhsT=vtok(b0 + bi)[:, g * 128:(g + 1) * 128], rhs=swT[:, g * 128:(g + 1) * 128],
                                      start=True, stop=False)
                            ins = pe.matmul(o, lhsT=lb2[0:2, g * 128:(g + 1) * 128],
                                            rhs=rb2[0:2, g * 128:(g + 1) * 128], start=False, stop=True)
                        return ins
                    alias["last_sgu"] = S.op("pe", fn, reads=[V[b0 + bi] for bi in range(nbk)] + [SWT, LB, RB2],
                                             writes=[PB[bm]])
                    bu = proj_group(sU, gg, hT, H, si, c0, n)
                    bz = proj_group(sZ, gg, hT, H, si, c0, n)
                    gu = nstg()
                    act(stg[:, gu, 0:n], ps[:, bu, 0:n], AF.Gelu_apprx_tanh, [PB[bu]], [STG[gu]])
                    th = nstg()
                    act(stg[:, th, 0:n], ps[:, bz, 0:n], AF.Tanh, [PB[bz]], [STG[th]], scale=0.5)
                    t = nstg()
                    tt(stg[:, t, 0:n], ps[:, bm, 0:n], stg[:, gu, 0:n], ALU.mult, [PB[bm], STG[gu]], [STG[t]])
                    t2 = nstg()
                    stt(stg[:, t2, 0:n], stg[:, th, 0:n], 1.0, stg[:, t, 0:n], ALU.add, ALU.mult,
                        [STG[th], STG[t]], [STG[t2]])
                    stt(yb[yi][:, g, c0:c0 + n], ps[:, bz, 0:n], 0.5, stg[:, t2, 0:n], ALU.mult, ALU.mult,
                        [PB[bz], STG[t2]], [Y[yi][g][si]])
            rel(sU)
            rel(sZ)

    def phase_f(l, br, yi, subs, mode):
        macc, M = cur["macc"], cur["M"]
        for hq in range(2):
            sG = acq()
            sW = acq()
            for ee in range(4):
                e = 4 * hq + ee
                for si, (c0, n) in enumerate(subs):
                    bg = proj_group(sG, ee, hT, H, si, c0, n)
                    bp = proj_group(sW, ee, yb[yi], Y[yi], si, c0, n)
                    sg = nstg()
                    act(stg[:, sg, 0:n], ps[:, bg, 0:n], AF.Sigmoid, [PB[bg]], [STG[sg]])
                    if mode == "first":
                        tt(macc[:, e, c0:c0 + n], ps[:, bp, 0:n], stg[:, sg, 0:n], ALU.mult,
                           [PB[bp], STG[sg]], [M[e][si]])
                    else:
                        t = nstg()
                        tt(stg[:, t, 0:n], ps[:, bp, 0:n], stg[:, sg, 0:n], ALU.mult, [PB[bp], STG[sg]], [STG[t]])
                        if mode == "mid":
                            tt(macc[:, e, c0:c0 + n], macc[:, e, c0:c0 + n], stg[:, t, 0:n], ALU.add,
                               [M[e][si], STG[t]], [M[e][si]])
                        else:
                            tt(mrg(e, c0, n), macc[:, e, c0:c0 + n], stg[:, t, 0:n], ALU.add,
                               [M[e][si], STG[t]], [MR[e][si]], extra=[alias["last_sgu"]])
            rel(sG)
            rel(sW)

    def phase_g(l, subs, nxt_l):
        xb, X = cur["xb"], cur["X"]
        sl = [acq(), acq()]

        def grp(si, e2):
            c0, n = subs[si]
            bk = bank()
            slot = sl[e2 // 4]
            cc = e2 % 4
            lhs = [ring[:, slot, k, cc * 128:(cc + 1) * 128] for k in range(8)]
            rhs = [mrg(k, c0, n) for k in range(8)]
            alias["last_g"] = mm_group(bk, n, lhs, rhs, [RB[slot]] + [MR[k][si] for k in range(8)])
            tt(xb[:, e2, c0:c0 + n], ps[:, bk, 0:n], xb[:, e2, c0:c0 + n], ALU.add,
               [PB[bk], X[e2][si]], [X[e2][si]])

        if len(subs) == 1:
            for e2 in range(8):
                grp(0, e2)
            rel(sl[0])
            rel(sl[1])
            if nxt_l is not None:
                rmsnorm(nxt_l, 0, subs[0][0], subs[0][1], False)
            return
        for e2 in range(8):
            grp(0, e2)
        grp(1, 0)
        grp(1, 1)
        rr0 = rms_stats(0, subs[0][0], subs[0][1]) if nxt_l is not None else None
        for e2 in range(2, 8):
            grp(1, e2)
            if e2 == 3:
                rel(sl[0])
            if nxt_l is not None and e2 <= 5:
                for k in (2 * (e2 - 2), 2 * (e2 - 2) + 1):
                    rms_apply(nxt_l, 0, subs[0][0], subs[0][1], False, rr0, k)
        rel(sl[1])
        if nxt_l is not None:
            rmsnorm(nxt_l, 1, subs[1][0], subs[1][1], False)

    pairs = [(ti_, l_) for ti_ in range(len(tiles)) for l_ in range(n_layers)]
    load_small_dma(0)
    load_small_finish(0)
    pump()

    def tile_subs(nb_, lo=0):
        ss = [(lo * 128, (min(4, nb_) - lo) * 128)]
        if nb_ > 4:
            ss.append((512, (nb_ - 4) * 128))
        return ss

    LO_BY_D = {0: (2, 1), 1: (1, None), 2: (1, 0), 3: (0, None)}

    def halo_plan(ti_, l_):
        if ti_ != 0 or not TRIM:
            return 0, None
        lo_, so_ = LO_BY_D[min(n_layers - 1 - l_, 3)]
        return lo_, (None if so_ is None else so_ * 128 + 112)

    toff = [0]
    for nb_ in tiles:
        toff.append(toff[-1] + nb_ * 128)

    def load_x(ti_):
        ntk_ = tiles[ti_] * 128
        t0_ = toff[ti_]
        dst = bufs2[ti_ % 2]
        S.op("sp", lambda q: q.dma_start(out=dst[:, :, 0:ntk_], in_=xT_v[:, :, t0_:t0_ + ntk_]),
             writes=[grids2[ti_ % 2][k][si] for k in range(8) for si in range(2)], sem="x")

    load_x(0)
    set_tile(0)
    for si, (c0, n) in enumerate(tile_subs(tiles[0], 0)):
        rmsnorm(0, si, c0, n, False)
    ocol = 0
    for ti, nb in enumerate(tiles):
        set_tile(ti)
        ntk = nb * 128
        for l in range(n_layers):
            lo, so_col = halo_plan(ti, l)
            subs = tile_subs(nb, lo)
            pi = pairs.index((ti, l))
            nl = pairs[pi + 1][1] if pi + 1 < len(pairs) else None
            last = l == n_layers - 1
            prod_c(l, 0, subs, so_col)
            prod_b(l, 1, subs, ti, so_col)
            phase_f(l, 2, 0, subs, "first")
            prod_a(l, 0, subs, nb, lo)
            if nl is not None:
                load_small_dma(nl)
            phase_f(l, 1, 1, subs, "mid")
            phase_f(l, 0, 0, subs, "last")
            if last and ti + 1 < len(tiles):
                load_x(ti + 1)
            if nl is not None:
                load_small_finish(nl)
            phase_g(l, subs, None if last else l + 1)
        if ti + 1 < len(tiles):
            set_tile(ti + 1)
            for si, (c0, n) in enumerate(tile_subs(tiles[ti + 1])):
                rmsnorm(0, si, c0, n, False)
            set_tile(ti)
        skip = out_skip if ti == 0 else 0
        xb_, X_ = cur["xb"], cur["X"]
        for si, (c0, n) in enumerate(subs):
            if final_norm:
                rmsnorm(n_layers - 1, si, c0, n, True)
            a0 = max(c0, skip)
            a1 = c0 + n
            S.op("sp", lambda q, a0=a0, a1=a1, oc=ocol + a0 - skip, xb_=xb_: q.dma_start(
                out=outT_v[:, :, oc:oc + a1 - a0], in_=xb_[:, :, a0:a1]),
                reads=[X_[k][si] for k in range(8)], sem="out%d" % si)
        ocol += ntk - skip

    keys = sorted(S.cnt.keys())
    sems = {k: es.enter_context(nc.semaphore("s_" + k)) for k in keys}
    with nc.Block() as block:
        def runner(eng):
            def f(h):
                for fn, waits, key, inc in S.q[eng]:
                    for k, n in waits:
                        h.wait_ge(sems[k], n)
                    fn(h).then_inc(sems[key], inc)
                if eng == "sp":
                    for kk in keys:
                        if kk.startswith("out"):
                            h.wait_ge(sems[kk], S.cnt[kk])
            return f
        block.tensor(runner("pe"))
        block.scalar(runner("act"))
        block.vector(runner("dve"))
        block.gpsimd(runner("pool"))
        block.sync(runner("sp"))
    import os
    if os.environ.get("KDEBUG"):
        print("sbuf bytes remaining", nc.sbuf_bytes_remaining, {e: len(q) for e, q in S.q.items()})
    es.close()
    return nc


def _pack_small(p, layers):
    def vec(a):
        return np.asarray(a, np.float32).reshape(8, 128).T
    pvec = np.zeros((128, len(layers), NV, 8), np.float32)
    for i, l in enumerate(layers):
        pvec[:, i, V_NG] = vec(p["norm_g"][l])
        pvec[:, i, V_LNG] = vec(p["sgu_ln_g"][l])
        pvec[:, i, V_PB] = vec(p["pool_b"][l])
        pvec[:, i, V_PS] = vec(p["pool_scale"][l])
        pvec[:, i, V_CW0] = vec(p["conv_w"][l, 0])
        pvec[:, i, V_CW1] = vec(p["conv_w"][l, 1])
        pvec[:, i, V_CW2] = vec(p["conv_w"][l, 2])
        pvec[:, i, V_CB] = vec(p["conv_b"][l])
    ls = list(layers)
    sgu_wT = np.ascontiguousarray(
        np.asarray(p["sgu_w"], np.float32)[ls].transpose(0, 3, 1, 2)).reshape(len(ls), 128, 1024)
    rows = np.stack([np.asarray(p["sgu_ln_b"], np.float32)[ls],
                     np.asarray(p["sgu_b"], np.float32)[ls].reshape(len(ls), 1024)], axis=1)
    return {
        "pvec": np.ascontiguousarray(pvec.reshape(128, -1)),
        "fing": np.ascontiguousarray(vec(p["final_g"])),
        "sgu_wT": sgu_wT,
        "rows": np.ascontiguousarray(rows),
        "lng_bc": np.ascontiguousarray(np.broadcast_to(
            np.asarray(p["sgu_ln_g"], np.float32)[ls][:, None, :], (len(ls), 128, 1024))),
        "pool_w": np.ascontiguousarray(np.asarray(p["pool_w"], np.float32)[ls]),
        "w_in": np.ascontiguousarray(np.asarray(p["w_in"], np.float32)[ls]),
        "w_br": np.ascontiguousarray(np.stack([np.asarray(p["w_branch_a"], np.float32)[ls],
                                               np.asarray(p["w_branch_b"], np.float32)[ls],
                                               np.asarray(p["w_branch_c"], np.float32)[ls]], axis=1)),
        "w_out": np.ascontiguousarray(np.asarray(p["w_out"], np.float32)[ls]),
    }


def _invcnt(start):
    t = np.zeros((4, 16), np.float32)
    for g, w in enumerate((2, 4, 8, 16)):
        for i in range(16):
            t[g, i] = 1.0 / (min(i + 1, w) if start else w)
    return np.ascontiguousarray(np.broadcast_to(t.reshape(1, 64), (128, 64)))


def _shard_x(x, n_seg, own):
    B = x.shape[0]
    halo = HALO_BLK * 128
    outs = []
    for b in range(B):
        for s in range(n_seg):
            s0 = s * own
            a = np.zeros((D, halo + own), np.float32)
            a[:, halo:] = x[b, s0:s0 + own].T
            if s > 0:
                a[:, :halo] = x[b, s0 - halo:s0].T
            outs.append(a)
    return outs


def _run(x, p, layers, final_norm, tiles, n_seg, own):
    small = _pack_small(p, layers)
    nc = build(tiles, len(layers), final_norm, len(layers))
    xs = _shard_x(x, n_seg, own)
    in_maps = []
    for c, xc in enumerate(xs):
        m = dict(small)
        m["xT"] = xc
        m["invcnt"] = _invcnt(c % n_seg == 0)
        in_maps.append(m)
    res = run_bass_kernel_spmd(nc, in_maps, core_ids=list(range(len(xs))))
    B = x.shape[0]
    out = np.empty((B, n_seg * own, D), np.float32)
    for c, r in enumerate(res.results):
        b, s = divmod(c, n_seg)
        out[b, s * own:(s + 1) * own] = r["outT"].T
    return out


TILES = [7, 7, 7, 7, 6]


def kernel(**inputs):
    p = {k: np.asarray(v) for k, v in inputs.items()}
    x = np.asarray(p.pop("x"), np.float32)
    if FUSED:
        return _run(x, p, list(range(L)), True, TILES, 4, 4096)
    for l in range(L):
        x = _run(x, p, [l], l == L - 1, TILES, 4, 4096)
    return x
```
